# Optimizing a Trainium2 kernel written in Bass

```python
import jax, jax.numpy as jnp
from jax import lax
import numpy as np

D_MODEL = 4096
BATCH = 8
SEQ = 2048
DEPTH = 2

GRID_W = 64
CTX_LEN = 256
HEAD_DIM = 128
ATTN_HEADS = D_MODEL // (2 * HEAD_DIM)
ATTN_KV_HEADS = ATTN_HEADS // 4
ATTN_GROUP = ATTN_HEADS // ATTN_KV_HEADS
WINDOW = 128
ATTN_BLOCK = 128
RET_V_DIM = 2 * HEAD_DIM
RET_HEADS = D_MODEL // (2 * RET_V_DIM)
RET_QK_DIM = HEAD_DIM
RET_CHUNK = 128
ATTN_WIDTH = ATTN_HEADS * HEAD_DIM
KV_WIDTH = ATTN_KV_HEADS * HEAD_DIM
RET_QK_WIDTH = RET_HEADS * RET_QK_DIM
RET_V_WIDTH = RET_HEADS * RET_V_DIM
MIX_WIDTH = ATTN_WIDTH + RET_V_WIDTH
IN_WIDTH = ATTN_WIDTH + 2 * KV_WIDTH + 2 * RET_QK_WIDTH + 2 * RET_V_WIDTH
D_FF = 4 * D_MODEL
ROPE_PAIRS = HEAD_DIM // 4
ROPE_BASE = 10000.0
EPS = 1e-6

kernel_name = 'hybrid_dit_swa_retention_block'

F32 = jnp.float32


def rmsnorm(t, g):
    tf = t.astype(F32)
    out = tf * lax.rsqrt(jnp.mean(tf * tf, axis=-1, keepdims=True) + EPS)
    return (out * g.astype(F32)).astype(t.dtype)


def modulate(h, shift, scale):
    return h * (1 + scale) + shift


def rope_2d(rows, length):
    r, cl = jnp.meshgrid(jnp.arange(rows), jnp.arange(GRID_W), indexing='ij')
    r = r.reshape(-1)[:length].astype(F32)
    cl = cl.reshape(-1)[:length].astype(F32)
    inv = ROPE_BASE ** (-jnp.arange(ROPE_PAIRS, dtype=F32) / ROPE_PAIRS)
    ar = r[:, None] * inv
    ac = cl[:, None] * inv
    ang = jnp.concatenate([ar, ar, ac, ac], axis=-1)
    return jnp.cos(ang), jnp.sin(ang)


def apply_rope(t, cos, sin):
    a1, a2, b1, b2 = jnp.split(t, 4, axis=-1)
    rot = jnp.concatenate([-a2, a1, -b2, b1], axis=-1)
    return t * cos.astype(t.dtype) + rot * sin.astype(t.dtype)


def to_heads(t, n_heads):
    b, l, _ = t.shape
    return t.reshape(b, l, n_heads, -1).transpose(0, 2, 1, 3)


def flip(t):
    return jnp.flip(t, axis=2)


def project(h, w_in, rope):
    s0 = ATTN_WIDTH
    s1 = s0 + KV_WIDTH
    s2 = s1 + KV_WIDTH
    s3 = s2 + RET_QK_WIDTH
    s4 = s3 + RET_QK_WIDTH
    s5 = s4 + RET_V_WIDTH
    p = h @ w_in
    aq, ak, av, rq, rk, rv, rg = jnp.split(p, [s0, s1, s2, s3, s4, s5], axis=-1)
    aq = to_heads(aq, ATTN_HEADS)
    ak = to_heads(ak, ATTN_KV_HEADS)
    av = to_heads(av, ATTN_KV_HEADS)
    rq = to_heads(rq, RET_HEADS)
    rk = to_heads(rk, RET_HEADS) * (RET_QK_DIM ** -0.5)
    rv = to_heads(rv, RET_HEADS)
    if rope is not None:
        cos, sin = rope
        aq = apply_rope(aq, cos, sin)
        ak = apply_rope(ak, cos, sin)
        rq = apply_rope(rq, cos, sin)
        rk = apply_rope(rk, cos, sin)
    return aq, ak, av, rq, rk, rv, rg


def window_attention(q, k, v, kc, vc, sink):
    b, hq, l, d = q.shape
    nb = l // ATTN_BLOCK
    nk = 3 * ATTN_BLOCK
    lc = kc.shape[2]
    qb = (q * (d ** -0.5)).reshape(b, ATTN_KV_HEADS, ATTN_GROUP, nb, ATTN_BLOCK, d)
    pad = ((0, 0), (0, 0), (ATTN_BLOCK, ATTN_BLOCK), (0, 0))
    kp = jnp.pad(k, pad).reshape(b, ATTN_KV_HEADS, nb + 2, ATTN_BLOCK, d)
    vp = jnp.pad(v, pad).reshape(b, ATTN_KV_HEADS, nb + 2, ATTN_BLOCK, d)
    kb = jnp.concatenate([kp[:, :, 0:nb], kp[:, :, 1:nb + 1], kp[:, :, 2:nb + 2]], axis=3)
    vb = jnp.concatenate([vp[:, :, 0:nb], vp[:, :, 1:nb + 1], vp[:, :, 2:nb + 2]], axis=3)
    qi = jnp.arange(nb)[:, None, None] * ATTN_BLOCK + jnp.arange(ATTN_BLOCK)[None, :, None]
    kj = (jnp.arange(nb)[:, None, None] - 1) * ATTN_BLOCK + jnp.arange(nk)[None, None, :]
    valid = (jnp.abs(kj - qi) <= WINDOW) & (kj >= 0) & (kj < l)
    s_loc = jnp.einsum('bhgnqd,bhnkd->bhgnqk', qb, kb).astype(F32)
    s_loc = jnp.where(valid, s_loc, -jnp.inf)
    s_ctx = jnp.einsum('bhgnqd,bhkd->bhgnqk', qb, kc).astype(F32)
    s_sink = jnp.broadcast_to(sink.astype(F32).reshape(ATTN_KV_HEADS, ATTN_GROUP, 1, 1, 1), s_loc.shape[:-1] + (1,))
    p = jax.nn.softmax(jnp.concatenate([s_loc, s_ctx, s_sink], axis=-1), axis=-1).astype(v.dtype)
    out = (jnp.einsum('bhgnqk,bhnkd->bhgnqd', p[..., :nk], vb)
           + jnp.einsum('bhgnqk,bhkd->bhgnqd', p[..., nk:nk + lc], vc))
    return out.reshape(b, hq, l, d)


def context_attention(q, k, v, sink):
    b, hq, lc, d = q.shape
    qg = (q * (d ** -0.5)).reshape(b, ATTN_KV_HEADS, ATTN_GROUP, lc, d)
    s = jnp.einsum('bhgqd,bhkd->bhgqk', qg, k).astype(F32)
    s_sink = jnp.broadcast_to(sink.astype(F32).reshape(ATTN_KV_HEADS, ATTN_GROUP, 1, 1), s.shape[:-1] + (1,))
    p = jax.nn.softmax(jnp.concatenate([s, s_sink], axis=-1), axis=-1).astype(v.dtype)
    out = jnp.einsum('bhgqk,bhkd->bhgqd', p[..., :lc], v)
    return out.reshape(b, hq, lc, d)


def retention_scan(q, k, v, log_gamma, state0, include_diag):
    b, h, l, dk = q.shape
    dv = v.shape[-1]
    n = l // RET_CHUNK
    dt = v.dtype
    lg = log_gamma.astype(F32)[:, None]
    pos = jnp.arange(RET_CHUNK, dtype=F32)
    rel = pos[:, None] - pos[None, :]
    mask = (rel >= 0) if include_diag else (rel > 0)
    decay_intra = jnp.where(mask, jnp.exp(lg[:, :, None] * jnp.maximum(rel, 0.0)), 0.0)
    q_decay = jnp.exp(lg * (pos + 1.0))
    k_decay = jnp.exp(lg * (RET_CHUNK - 1.0 - pos))
    chunk_decay = jnp.exp(lg[:, 0] * RET_CHUNK)
    qc = q.reshape(b, h, n, RET_CHUNK, dk)
    kc = k.reshape(b, h, n, RET_CHUNK, dk)
    vc = v.reshape(b, h, n, RET_CHUNK, dv)
    scores = jnp.einsum('bhncd,bhnsd->bhncs', qc, kc) * decay_intra[:, None].astype(dt)
    intra = jnp.einsum('bhncs,bhnse->bhnce', scores, vc)
    kv = jnp.einsum('bhnsd,bhnse->nbhde', kc * k_decay[:, None, :, None].astype(dt), vc)
    cd = chunk_decay[:, None, None].astype(state0.dtype)

    def step(s, kv_c):
        return s * cd + kv_c, s

    s_final, s_before = lax.scan(step, state0, kv)
    cross = jnp.einsum('bhncd,nbhde->bhnce', qc * q_decay[:, None, :, None].astype(dt), s_before)
    return (intra + cross).reshape(b, h, l, dv), s_final


def retention_output(y, gate, gn_g):
    b, h, l, dv = y.shape
    yf = y.astype(F32)
    mu = jnp.mean(yf, axis=-1, keepdims=True)
    var = jnp.mean(jnp.square(yf - mu), axis=-1, keepdims=True)
    yn = (yf - mu) * lax.rsqrt(var + EPS) * gn_g.astype(F32)[:, None, :]
    yn = yn.astype(y.dtype).transpose(0, 2, 1, 3).reshape(b, l, h * dv)
    return jax.nn.silu(gate) * yn


def merge(attn_o, ret_o, w_out):
    b, h, l, d = attn_o.shape
    a = attn_o.transpose(0, 2, 1, 3).reshape(b, l, h * d)
    return jnp.concatenate([a, ret_o], axis=-1) @ w_out


def sqrelu_mlp(h, w_up, w_down):
    return jnp.square(jax.nn.relu(h @ w_up)) @ w_down


def setup_inputs(seed: int = 0) -> dict:
    key = jax.random.key(seed)
    ks = jax.random.split(key, 14)
    nrm = jax.random.normal
    x = nrm(ks[0], (BATCH, SEQ, D_MODEL), F32)
    c = nrm(ks[1], (BATCH, D_MODEL), F32)
    ctx = nrm(ks[2], (BATCH, CTX_LEN, D_MODEL), F32)
    c_ctx = nrm(ks[3], (D_MODEL,), F32)
    w_ada = nrm(ks[4], (DEPTH, D_MODEL, 6 * D_MODEL), F32) * (0.5 * D_MODEL ** -0.5)
    b_ada = 0.02 * nrm(ks[5], (DEPTH, 6 * D_MODEL), F32)
    norm_g = 1.0 + 0.02 * nrm(ks[6], (DEPTH, 4, D_MODEL), F32)
    w_in = nrm(ks[7], (DEPTH, D_MODEL, IN_WIDTH), F32) * (D_MODEL ** -0.5)
    attn_sink = 0.5 * nrm(ks[8], (DEPTH, ATTN_HEADS), F32)
    base = jnp.log1p(-jnp.exp2(-5.0 - jnp.arange(RET_HEADS, dtype=F32)))
    ret_log_decay = base * (1.0 + 0.1 * nrm(ks[9], (DEPTH, 2, RET_HEADS), F32))
    ret_gn_g = 1.0 + 0.02 * nrm(ks[10], (DEPTH, RET_HEADS, RET_V_DIM), F32)
    w_out = nrm(ks[11], (DEPTH, MIX_WIDTH, D_MODEL), F32) * (MIX_WIDTH ** -0.5)
    w_up = nrm(ks[12], (DEPTH, D_MODEL, D_FF), F32) * (D_MODEL ** -0.5)
    w_down = nrm(ks[13], (DEPTH, D_FF, D_MODEL), F32) * (D_FF ** -0.5)
    return {'x': x, 'c': c, 'ctx': ctx, 'c_ctx': c_ctx, 'w_ada': w_ada, 'b_ada': b_ada, 'norm_g': norm_g,
            'w_in': w_in, 'attn_sink': attn_sink, 'ret_log_decay': ret_log_decay, 'ret_gn_g': ret_gn_g,
            'w_out': w_out, 'w_up': w_up, 'w_down': w_down}


def reference(x, c, ctx, c_ctx, w_ada, b_ada, norm_g, w_in, attn_sink, ret_log_decay, ret_gn_g, w_out, w_up, w_down):
    b, l, _ = x.shape
    ROWS = l // GRID_W
    rope = rope_2d(ROWS, l)
    xc = ctx
    for li in range(DEPTH):
        last = li == DEPTH - 1
        g_pre_mix, g_post_mix, g_pre_mlp, g_post_mlp = norm_g[li]
        sh1, sc1, gt1, sh2, sc2, gt2 = [m[:, None, :] for m in jnp.split(jax.nn.silu(c) @ w_ada[li] + b_ada[li], 6, axis=-1)]
        csh1, csc1, cgt1, csh2, csc2, cgt2 = jnp.split(jax.nn.silu(c_ctx) @ w_ada[li] + b_ada[li], 6, axis=-1)
        lg_f = ret_log_decay[li, 0]
        lg_b = ret_log_decay[li, 1]

        h = modulate(rmsnorm(x, g_pre_mix), sh1, sc1)
        hc = modulate(rmsnorm(xc, g_pre_mix), csh1, csc1)
        aq, ak, av, rq, rk, rv, rg = project(h, w_in[li], rope)
        caq, cak, cav, crq, crk, crv, crg = project(hc, w_in[li], None)

        zeros = jnp.zeros((b, RET_HEADS, RET_QK_DIM, RET_V_DIM), rv.dtype)
        cr_f, s_f = retention_scan(crq, crk, crv, lg_f, zeros, True)
        cr_b, s_b = retention_scan(flip(crq), flip(crk), flip(crv), lg_b, zeros, False)
        r_f, _ = retention_scan(rq, rk, rv, lg_f, s_f, True)
        r_b, _ = retention_scan(flip(rq), flip(rk), flip(rv), lg_b, s_b, False)
        ret = retention_output(r_f + flip(r_b), rg, ret_gn_g[li])
        att = window_attention(aq, ak, av, cak, cav, attn_sink[li])
        y = merge(att, ret, w_out[li])
        x = x + gt1 * rmsnorm(y, g_post_mix)
        if not last:
            catt = context_attention(caq, cak, cav, attn_sink[li])
            cret = retention_output(cr_f + flip(cr_b), crg, ret_gn_g[li])
            xc = xc + cgt1 * rmsnorm(merge(catt, cret, w_out[li]), g_post_mix)

        h = modulate(rmsnorm(x, g_pre_mlp), sh2, sc2)
        x = x + gt2 * rmsnorm(sqrelu_mlp(h, w_up[li], w_down[li]), g_post_mlp)
        if not last:
            hc = modulate(rmsnorm(xc, g_pre_mlp), csh2, csc2)
            xc = xc + cgt2 * rmsnorm(sqrelu_mlp(hc, w_up[li], w_down[li]), g_post_mlp)
    return x
```

```python
from contextlib import ExitStack
import numpy as np
import ml_dtypes
import concourse.bass as bass
import concourse.mybir as mybir
from concourse.bass_utils import run_bass_kernel_spmd

F32 = mybir.dt.float32
BF16 = mybir.dt.bfloat16
AF = mybir.ActivationFunctionType
ALU = mybir.AluOpType
AX = mybir.AxisListType

NT = 18
TOK = 2304
D = 4096
KC = 32
DFF = 16384
EPS = 1e-6
DEPTH = 2
SCALE = 128.0 ** -0.5
NDUMMY = 0
PRECOUNT = 0


class Res:
    __slots__ = ("w", "r", "ro")

    def __init__(self, ro=False):
        self.w = {}
        self.r = {}
        self.ro = ro


class Eng:
    def __init__(self, fw, eng, name):
        self.eng = eng
        self.name = name
        self.sem = fw.nc.alloc_semaphore("s_" + name)
        fw.sems[self.sem.num] = self.sem
        self.count = 0
        self.known = {}


class FW:
    def __init__(self, nc):
        self.nc = nc
        self.sems = {}
        self.pe = Eng(self, nc.tensor, "pe")
        self.act = Eng(self, nc.scalar, "act")
        self.dve = Eng(self, nc.vector, "dve")
        self.pool = Eng(self, nc.gpsimd, "pool")
        self.sp = Eng(self, nc.sync, "sp")
        self.engs = [self.pe, self.act, self.dve, self.pool, self.sp]
        if PRECOUNT:
            for _ in range(PRECOUNT // 1000):
                self.dve.eng.sem_inc(self.dve.sem, 1000)
            self.dve.count = PRECOUNT
        self.dsem = {}
        self.free_dsems = []
        for i in range(NDUMMY):
            nc.alloc_semaphore(f"dummy{i}")
        self.bar = nc.alloc_semaphore("s_bar")
        self.barcount = 0
        self.ninst = 0

    def _deps(self, reads, writes):
        deps = {}
        for r in reads:
            for s, v in r.w.items():
                if deps.get(s, 0) < v:
                    deps[s] = v
        for w in writes:
            for s, v in w.w.items():
                if deps.get(s, 0) < v:
                    deps[s] = v
            for s, v in w.r.items():
                if deps.get(s, 0) < v:
                    deps[s] = v
        return deps

    def _wait(self, E, deps, skip_own=False):
        for num, val in deps.items():
            if skip_own and num == E.sem.num:
                continue
            if E.known.get(num, 0) >= val:
                continue
            E.eng.wait_ge(self.sems[num], val)
            E.known[num] = val
            self.ninst += 1

    def _finish(self, ev, reads, writes):
        for r in reads:
            if not r.ro and r.r.get(ev[0], 0) < ev[1]:
                r.r[ev[0]] = ev[1]
        for w in writes:
            if w.w.get(ev[0], 0) < ev[1]:
                w.w[ev[0]] = ev[1]

    def op(self, E, fn, reads=(), writes=()):
        self._wait(E, self._deps(reads, writes))
        inst = fn(E.eng)
        E.count += 1
        inst.then_inc(E.sem, 1)
        self.ninst += 1
        self._finish((E.sem.num, E.count), reads, writes)

    def mm(self, out_res, mms, reads, transpose=False):
        E = self.pe
        self._wait(E, self._deps(reads, [out_res]), skip_own=True)
        inst = None
        for (o, a, b, st, sp) in mms:
            if transpose:
                inst = E.eng.transpose(out=o, in_=a, identity=b)
            else:
                inst = E.eng.matmul(o, lhsT=a, rhs=b, start=st, stop=sp)
            self.ninst += 1
        E.count += 1
        inst.then_inc(E.sem, 1)
        self._finish((E.sem.num, E.count), reads, [out_res])

    def dma(self, Q, key, pairs, reads=(), writes=(), **kw):
        if key not in self.dsem:
            if self.free_dsems:
                self.dsem[key] = self.free_dsems.pop()
            else:
                sem = self.nc.alloc_semaphore(U("d_" + key))
                self.sems[sem.num] = sem
                self.dsem[key] = [sem, 0]
        sem, cnt = self.dsem[key]
        deps = self._deps(reads, writes)
        if cnt > 0 and deps.get(sem.num, 0) < cnt:
            deps[sem.num] = cnt
        self._wait(Q, deps)
        for (o, i) in pairs:
            Q.eng.dma_start(out=o, in_=i, **kw).then_inc(sem, 16)
            cnt += 16
            self.ninst += 1
        self.dsem[key][1] = cnt
        self._finish((sem.num, cnt), reads, writes)

    def collective(self, key, kind, ins, outs, rg, reads=(), writes=()):
        Q = self.pool
        if key not in self.dsem:
            if self.free_dsems:
                self.dsem[key] = self.free_dsems.pop()
            else:
                sem = self.nc.alloc_semaphore(U("c_" + key))
                self.sems[sem.num] = sem
                self.dsem[key] = [sem, 0]
        sem, cnt = self.dsem[key]
        deps = self._deps(reads, writes)
        if cnt > 0 and deps.get(sem.num, 0) < cnt:
            deps[sem.num] = cnt
        self._wait(Q, deps)
        Q.eng.collective_compute(kind, ALU.bypass, replica_groups=rg, ins=ins, outs=outs).then_inc(sem, 1)
        cnt += 1
        self.ninst += 1
        self.dsem[key][1] = cnt
        self._finish((sem.num, cnt), reads, writes)

    def barrier(self, final=False):
        deps = {E.sem.num: E.count for E in self.engs if E.count > 0}
        for key, (sem, cnt) in self.dsem.items():
            if cnt > 0 and (final or not key.startswith(("cv", "ag"))):
                deps[sem.num] = cnt
        self._wait(self.sp, deps)
        if not final:
            for key in list(self.dsem):
                if not key.startswith(("cv", "ag")):
                    self.free_dsems.append(self.dsem.pop(key))
        self.barcount += 1
        self.sp.eng.sem_inc(self.bar, 1)
        for E in self.engs:
            if E is not self.sp:
                E.eng.wait_ge(self.bar, self.barcount)
            for k, v in deps.items():
                E.known[k] = max(E.known.get(k, 0), v)


def _consts():
    f = np.float32
    pos = np.arange(2048)
    r = (pos // 64).astype(f)
    cl = (pos % 64).astype(f)
    inv = (10000.0 ** (-np.arange(32, dtype=f) / 32)).astype(f)
    ar = r[:, None] * inv
    ac = cl[:, None] * inv
    ang = np.concatenate([ar, ar, ac, ac], -1)
    cos = np.cos(ang).astype(f)
    sin = np.sin(ang).astype(f)
    sgn = np.concatenate([-np.ones(32), np.ones(32), -np.ones(32), np.ones(32)]).astype(f)
    cosT = np.ones((128, TOK), f)
    sinT = np.zeros((128, TOK), f)
    cosT[:, 256:] = cos.T
    sinT[:, 256:] = (sin * sgn[None, :]).T
    p = np.arange(128)
    kk = p[:, None]
    qq = p[None, :]
    mprev = np.tile((kk >= qq).astype(f), (1, 4)).astype(ml_dtypes.bfloat16)
    mnext = np.tile((kk <= qq).astype(f), (1, 4)).astype(ml_dtypes.bfloat16)
    s_ = p[:, None]
    c_ = p[None, :]
    relF = np.maximum(c_ - s_, 0).astype(f)
    mF = (c_ >= s_).astype(f)
    relB = np.maximum(s_ - c_, 0).astype(f)
    mB = (c_ < s_).astype(f)
    rows = np.stack([np.broadcast_to((c_ + 1).astype(f), (128, 128)),
                     np.broadcast_to((128 - c_).astype(f), (128, 128))], 0)
    rett = np.ascontiguousarray(np.stack([relF, mF * SCALE, relB, mB * SCALE, rows[0], rows[1]], 1)).astype(f)
    pcol = np.stack([127 - p, p, np.full(128, 128)], 1).astype(f)
    ident = np.eye(128, dtype=f).astype(ml_dtypes.bfloat16)
    ones = np.ones((128, 128), f).astype(ml_dtypes.bfloat16)
    return dict(cosT=cosT, sinT=sinT, mprev=mprev, mnext=mnext, rett=rett, pcol=pcol, ident=ident, onesb=ones)


class Ctx:
    pass


_UN = [0]


def U(name):
    _UN[0] += 1
    return f"{name}_{_UN[0]}"


def build(stop_after=None, dump=(), inject=(), only=None):
    nc = bass.Bass("TRN2", target_bir_lowering=False)
    fw = FW(nc)
    C = Ctx()
    C.nc = nc
    C.fw = fw
    dump = set(dump)

    def dram_in(name, shape, dt):
        return nc.dram_tensor(name, list(shape), dt, kind="ExternalInput").ap()

    inject = set(inject)

    def scratch(name, shape, dt):
        kind = "ExternalOutput" if name in dump else ("ExternalInput" if name in inject else "Internal")
        return nc.dram_tensor(name, list(shape), dt, kind=kind).ap()

    C.x_in = dram_in("x", [2048, D], F32)
    C.ctx_in = dram_in("ctx", [256, D], F32)
    C.c9T = dram_in("c9T", [128, KC, 9], F32)
    C.sel = dram_in("sel", [9, 2], F32)
    small = only is not None and "convert" not in only
    C.w_ada = dram_in("w_ada_sh", [DEPTH, 128 if small else D, 3072], F32)
    C.b_ada = dram_in("b_ada_sh", [DEPTH, 3072], F32)
    C.norm_g = dram_in("norm_g", [DEPTH, 4, D], F32)
    C.norm_gT = dram_in("norm_gT", [DEPTH, 4, 128, KC], F32)
    C.w_in = dram_in("w_in_sh", [DEPTH, 128 if small else D // 8, 9216], F32)
    C.attn_sink = dram_in("attn_sink", [DEPTH, 16], F32)
    C.ret_log_decay = dram_in("ret_log_decay", [DEPTH, 16], F32)
    C.ret_gn_g = dram_in("ret_gn_g", [DEPTH, 8, 256], F32)
    C.w_out = dram_in("w_out_sh", [DEPTH, 128 if small else D // 8, D], F32)
    C.w_up = dram_in("w_up_sh", [DEPTH, 128 if small else D // 8, DFF], F32)
    C.w_down = dram_in("w_down_sh", [DEPTH, 128 if small else DFF // 8, D], F32)
    C.k_cosT = dram_in("cosT", [128, TOK], F32)
    C.k_sinT = dram_in("sinT", [128, TOK], F32)
    C.k_mprev = dram_in("mprev", [128, 512], BF16)
    C.k_mnext = dram_in("mnext", [128, 512], BF16)
    C.k_rett = dram_in("rett", [128, 6, 128], F32)
    C.k_pcol = dram_in("pcol", [128, 3], F32)
    C.k_ident = dram_in("ident", [128, 128], BF16)
    C.k_onesb = dram_in("onesb", [128, 128], BF16)
    C.out = nc.dram_tensor("out", [2048, D], F32, kind="ExternalOutput").ap()

    WSPEC = [("w_in", D, 9216), ("w_out", D, D), ("w_up", D, DFF), ("w_down", DFF, D)]
    C.wspec = WSPEC
    C.wsh = {k: nc.dram_tensor(k + "_shb", [DEPTH, rows // 8, cols], BF16, kind="Internal").ap() for k, rows, cols in WSPEC}
    C.wb = {k: [(nc.dram_tensor(f"{k}_b{li}", [rows, cols], BF16, kind="ExternalInput").ap() if f"{k}_b{li}" in inject else
                 nc.dram_tensor(f"{k}_b{li}", [rows, cols], BF16, kind="Internal", addr_space="Shared").ap()) for li in range(DEPTH)]
            for k, rows, cols in WSPEC}
    C.wres = {k: [[Res(ro=True) for _ in range(rows // 1024)] for _ in range(DEPTH)] for k, rows, cols in WSPEC}
    C.modp = nc.dram_tensor("modp", [DEPTH, 9, 3072], F32, kind="Internal").ap()
    C.modg = [nc.dram_tensor(f"modg{li}", [72, 3072], F32, kind="Internal", addr_space="Shared").ap() for li in range(DEPTH)]
    C.mod_s = scratch("mod_s", [DEPTH, 2, 6 * D], F32)
    C.mod_res = [Res() for _ in range(DEPTH)]
    C.hT_s = scratch("hT_s", [NT, 128, KC, 128], BF16)
    C.hT_res = [Res() for _ in range(NT)]
    C.h2T_s = scratch("h2T_s", [NT, 128, KC, 128], BF16)
    C.h2T_res = [Res() for _ in range(NT)]
    C.mixT_s = scratch("mixT_s", [NT, 128, KC, 128], BF16)
    C.mix_res = [Res() for _ in range(NT)]
    C.aqT_s = scratch("aqT_s", [16, 128, TOK], BF16)
    C.akT_s = scratch("akT_s", [4, 128, TOK], BF16)
    C.rqT_s = scratch("rqT_s", [8, 128, TOK], BF16)
    C.rkT_s = scratch("rkT_s", [8, 128, TOK], BF16)
    C.av_s = scratch("av_s", [TOK, 512], BF16)
    C.rv_s = scratch("rv_s", [TOK, 2048], BF16)
    C.sg_s = scratch("sg_s", [TOK, 2048], BF16)
    C.proj_res = Res()
    C.y_s = scratch("y_s", [TOK, D], F32)
    C.y_res = [Res() for _ in range(NT)]
    C.xmid_s = scratch("xmid_s", [TOK, D], F32)
    C.xmid_res = [Res() for _ in range(NT)]
    C.xnext_s = scratch("xnext_s", [TOK, D], F32)
    C.xnext_res = [Res() for _ in range(NT)]

    phases = []

    def run(name, fn, *a):
        if C.stopped or (only is not None and name not in only):
            return
        fn(C, *a)
        fw.barrier()
        if stop_after == name:
            C.stopped = True

    C.stopped = False
    with ExitStack() as _es:
        ident = _es.enter_context(nc.sbuf_tensor(U("ident"), [128, 128], BF16))
        ssq = _es.enter_context(nc.sbuf_tensor(U("ssq"), [128, NT, 8], F32))
        C.ident = ident
        C.ident_res = Res(ro=True)
        C.ssq = ssq
        C.ssq_res = Res()
        fw.dma(fw.sp, "const0", [(ident[:], C.k_ident)], writes=[C.ident_res])
        if "ssq_in" in inject:
            fw.dma(fw.sp, "const1", [(ssq[:], dram_in("ssq_in", [128, NT, 8], F32))], writes=[C.ssq_res])
        run("convert", phase_convert, 0, 4)
        run("mod", phase_mod)
        run("convert", phase_convert, 4, None)
        for li in range(DEPTH):
            last = li == DEPTH - 1
            C.li = li
            C.last = last
            C.tiles = list(range(2, NT)) if last else list(range(NT))
            if li == 0:
                run("norm1_0", phase_resid_norm, None, None, None, 0, C.hT_s, C.hT_res, list(range(NT)))
            run(f"inproj_{li}", phase_inproj)
            run(f"ret_{li}", phase_ret)
            run(f"att_{li}", phase_att)
            run(f"outproj_{li}", phase_proj_out, "w_out", C.mixT_s, C.mix_res)
            run(f"resid1_{li}", phase_resid_norm, "x", C.xmid_s, C.xmid_res, 1, C.h2T_s, C.h2T_res, C.tiles)
            run(f"mlp_{li}", phase_mlp)
            if not last:
                run(f"resid2_{li}", phase_resid_norm, "xmid", C.xnext_s, C.xnext_res, 2, C.hT_s, C.hT_res, C.tiles)
            else:
                run(f"resid2_{li}", phase_resid_norm, "xmid", "out", None, 2, None, None, C.tiles)
        fw.barrier(final=True)
    return nc


RG = [list(range(8))]


def kperm(name, s_):
    rs = (DFF // 8 if name == "w_down" else D // 8) // 128
    return (s_ % 8) * rs + s_ // 8


def phase_convert(C, lo, hi):
    fw = C.fw
    srcs = {"w_in": C.w_in, "w_out": C.w_out, "w_up": C.w_up, "w_down": C.w_down}
    items = [(li, name, c) for li in range(DEPTH) for name, rows, cols in C.wspec for c in range(rows // 1024)]
    for i, (li, name, c) in enumerate(items):
        if not (lo <= i < (hi if hi is not None else len(items))):
            continue
        src, sh, dst = srcs[name], C.wsh[name], C.wb[name][li]
        r_sh = Res()
        fw.dma(fw.pool, f"cv{i % 8}", [(sh[li, c * 128:(c + 1) * 128, :], src[li, c * 128:(c + 1) * 128, :])], writes=[r_sh])
        fw.collective(f"ag{i % 8}", "AllGather", [sh[li, c * 128:(c + 1) * 128, :]], [dst[c * 1024:(c + 1) * 1024, :]], RG,
                      reads=[r_sh], writes=[C.wres[name][li][c]])


def phase_mod(C):
    nc, fw = C.nc, C.fw
    with ExitStack() as _es:
        cT = _es.enter_context(nc.sbuf_tensor(U("m_c"), [128, KC, 9], F32))
        cs = _es.enter_context(nc.sbuf_tensor(U("m_cs"), [128, KC, 9], F32))
        wt = _es.enter_context(nc.sbuf_tensor(U("m_w"), [128, 2, KC, 512], F32))
        bt = _es.enter_context(nc.sbuf_tensor(U("m_b"), [9, 2, 512], F32))
        ot = _es.enter_context(nc.sbuf_tensor(U("m_o"), [9, 2, 512], F32))
        sel = _es.enter_context(nc.sbuf_tensor(U("m_sel"), [9, 2], F32))
        gt = _es.enter_context(nc.sbuf_tensor(U("m_g"), [9, 2, 3072], F32))
        o2 = _es.enter_context(nc.sbuf_tensor(U("m_o2"), [2, 2, 512], F32))
        ps0 = _es.enter_context(nc.psum_tensor(U("m_ps0"), [128, 512], F32))
        ps1 = _es.enter_context(nc.psum_tensor(U("m_ps1"), [128, 512], F32))
        r_c, r_cs, r_sel = Res(), Res(), Res()
        r_w, r_b, r_o, r_ps, r_g, r_o2 = [Res(), Res()], [Res(), Res()], [Res(), Res()], [Res(), Res()], [Res(), Res()], [Res(), Res()]
        r_modp = [Res() for _ in range(DEPTH)]
        r_modg = [Res() for _ in range(DEPTH)]
        pss = [ps0, ps1]
        fw.dma(fw.sp, "m_c", [(cT[:], C.c9T), (sel[:], C.sel)], writes=[r_c, r_sel])
        fw.op(fw.act, lambda e: e.activation(out=cs[:], in_=cT[:], func=AF.Silu), reads=[r_c], writes=[r_cs])
        n = 0
        for li in range(DEPTH):
            for cb in range(6):
                s = n % 2
                n += 1
                cols = slice(cb * 512, (cb + 1) * 512)
                fw.dma(fw.sp, f"m_w{s}", [(wt[:, s], C.w_ada[li, :, cols].rearrange("(k p) n -> p k n", p=128))], writes=[r_w[s]])
                fw.dma(fw.sp, f"m_b{s}", [(bt[:, s, :], C.b_ada[li, cols].partition_broadcast(9))], writes=[r_b[s]])
                fw.mm(r_ps[s], [(pss[s][0:9, :], cs[:, kc, :], wt[:, s, kc, :], kc == 0, kc == KC - 1) for kc in range(KC)],
                      reads=[r_cs, r_w[s]])
                fw.op(fw.dve, lambda e: e.tensor_tensor(out=ot[:, s, :], in0=pss[s][0:9, :], in1=bt[:, s, :], op=ALU.add),
                      reads=[r_ps[s], r_b[s]], writes=[r_o[s]])
                fw.dma(fw.sp, f"m_o{s}", [(C.modp[li, :, cols], ot[:, s, :])], reads=[r_o[s]], writes=[r_modp[li]])
            fw.collective(f"mg{li}", "AllGather", [C.modp[li]], [C.modg[li]], RG, reads=[r_modp[li]], writes=[r_modg[li]])
        m = 0
        for li in range(DEPTH):
            for r in range(8):
                g = r % 2
                fw.dma(fw.sp, f"m_g{g}", [(gt[:, g, :], C.modg[li][r * 9:(r + 1) * 9, :])], reads=[r_modg[li]], writes=[r_g[g]])
                for j in range(6):
                    s = m % 2
                    m += 1
                    fw.mm(r_ps[s], [(pss[s][0:2, :], sel[:], gt[:, g, j * 512:(j + 1) * 512], True, True)], reads=[r_sel, r_g[g]])
                    fw.op(fw.dve, lambda e: e.tensor_copy(out=o2[:, s, :], in_=pss[s][0:2, :]), reads=[r_ps[s]], writes=[r_o2[s]])
                    fw.dma(fw.sp, f"m_p{s}", [(C.mod_s[li, :, j * D + r * 512:j * D + (r + 1) * 512], o2[:, s, :])], reads=[r_o2[s]],
                           writes=[C.mod_res[li]])


def load_mod_tiles(C, which, AB, r_AB, G, r_G, tmp, r_tmp):
    nc, fw = C.nc, C.fw
    li = C.li
    if which == 0:
        gi, (mli, shi, sci, ng) = None, (li, 0, 1, 0)
    elif which == 1:
        gi, (mli, shi, sci, ng) = (li, 2, 1), (li, 3, 4, 2)
    else:
        gi = (li, 5, 3)
        mli, shi, sci, ng = (li + 1, 0, 1, 0) if li + 1 < DEPTH else (None, 0, 0, 0)
    if mli is not None:
        pairs = []
        for row in range(2):
            pairs.append((AB[:, row, 1, :], C.mod_s[mli, row, shi * D:(shi + 1) * D].rearrange("(k p) -> p k", p=128)))
            pairs.append((AB[:, row, 0, :], C.mod_s[mli, row, sci * D:(sci + 1) * D].rearrange("(k p) -> p k", p=128)))
        fw.dma(fw.sp, "mt_ab", pairs, reads=[C.mod_res[mli]], writes=[r_AB], allow_slow_non_contiguous=True)
        fw.dma(fw.sp, "mt_g", [(tmp[:, 0:KC], C.norm_gT[mli, ng])], writes=[r_tmp])
        for row in range(2):
            fw.op(fw.dve, lambda e: e.scalar_tensor_tensor(out=AB[:, row, 0, :], in0=AB[:, row, 0, :], scalar=1.0, in1=tmp[:, 0:KC],
                                                            op0=ALU.add, op1=ALU.mult), reads=[r_tmp], writes=[r_AB])
    if gi is not None:
        gli, gti, gng = gi
        for row in range(2):
            fw.dma(fw.sp, "mt_G", [(G[:, row, :], C.mod_s[gli, row, gti * D:(gti + 1) * D].partition_broadcast(128))],
                   reads=[C.mod_res[gli]], writes=[r_G])
        fw.dma(fw.sp, "mt_g2", [(tmp[:], C.norm_g[gli, gng].partition_broadcast(128))], reads=[], writes=[r_tmp])
        for row in range(2):
            fw.op(fw.dve, lambda e: e.tensor_tensor(out=G[:, row, :], in0=G[:, row, :], in1=tmp[:], op=ALU.mult),
                  reads=[r_tmp], writes=[r_G])


def x_src(C, which, t):
    rows = slice(t * 128, (t + 1) * 128)
    if which == "x":
        if C.li == 0:
            return (C.ctx_in[rows, :], None) if t < 2 else (C.x_in[(t - 2) * 128:(t - 1) * 128, :], None)
        return C.xnext_s[rows, :], C.xnext_res[t]
    return C.xmid_s[rows, :], C.xmid_res[t]


def phase_resid_norm(C, xwhich, xdst, xdst_res, which, hdst, hdst_res, tiles):
    nc, fw = C.nc, C.fw
    li = C.li
    resid = xwhich is not None
    do_norm = hdst is not None
    with ExitStack() as _es:
        AB = _es.enter_context(nc.sbuf_tensor(U("rn_AB"), [128, 2, 2, KC], F32))
        G = _es.enter_context(nc.sbuf_tensor(U("rn_G"), [128, 2, D], F32))
        tmp = _es.enter_context(nc.sbuf_tensor(U("rn_tmp"), [128, D], F32))
        xt = _es.enter_context(nc.sbuf_tensor(U("rn_x"), [128, 2, D], F32))
        yt = _es.enter_context(nc.sbuf_tensor(U("rn_y"), [128, 2, D], F32))
        junk = _es.enter_context(nc.sbuf_tensor(U("rn_junk"), [128, D], BF16))
        xs = _es.enter_context(nc.sbuf_tensor(U("rn_xs"), [128, 2, D], BF16))
        hT = _es.enter_context(nc.sbuf_tensor(U("rn_hT"), [128, 2, KC, 128], BF16))
        st = _es.enter_context(nc.sbuf_tensor(U("rn_st"), [128, 2, 8], F32))
        p0 = _es.enter_context(nc.psum_tensor(U("rn_p0"), [128, 1024], BF16))
        p1 = _es.enter_context(nc.psum_tensor(U("rn_p1"), [128, 1024], BF16))
        r_AB, r_G, r_tmp, r_junk = Res(), Res(), Res(), Res()
        r_x, r_y, r_xs, r_hT, r_st = [Res(), Res()], [Res(), Res()], [Res(), Res()], [Res(), Res()], [Res(), Res()]
        r_p = [Res(), Res()]
        pp = [p0, p1]
        load_mod_tiles(C, which, AB, r_AB, G if resid else None, r_G, tmp, r_tmp)
        def issue_loads(n):
            t = tiles[n]
            s = n % 2
            rows = slice(t * 128, (t + 1) * 128)
            if resid:
                xap, xres = x_src(C, xwhich, t)
                fw.dma(fw.sp, f"rn_x{s}", [(xt[:, s, :], xap)], reads=[xres] if xres else [], writes=[r_x[s]])
                fw.dma(fw.sp, f"rn_y{s}", [(yt[:, s, :], C.y_s[rows, :])], reads=[C.y_res[t]], writes=[r_y[s]])
            else:
                xap = C.ctx_in[rows, :] if t < 2 else C.x_in[(t - 2) * 128:(t - 1) * 128, :]
                fw.dma(fw.sp, f"rn_x{s}", [(xt[:, s, :], xap)], writes=[r_x[s]])

        issue_loads(0)
        for n, t in enumerate(tiles):
            s = n % 2
            row = 0 if t >= 2 else 1
            rows = slice(t * 128, (t + 1) * 128)
            if n + 1 < len(tiles):
                issue_loads(n + 1)
            if resid:
                fw.op(fw.dve, lambda e: e.tensor_reduce(out=st[:, s, 0:1], in_=C.ssq[:, t, :], axis=AX.X, op=ALU.add),
                      reads=[C.ssq_res], writes=[r_st[s]])
                fw.op(fw.dve, lambda e: e.tensor_scalar(out=st[:, s, 1:2], in0=st[:, s, 0:1], scalar1=1.0 / D, scalar2=EPS, op0=ALU.mult, op1=ALU.add),
                      reads=[r_st[s]], writes=[r_st[s]])
                fw.op(fw.act, lambda e: e.activation(out=st[:, s, 2:3], in_=st[:, s, 1:2], func=AF.Sqrt), reads=[r_st[s]], writes=[r_st[s]])
                fw.op(fw.dve, lambda e: e.reciprocal(out=st[:, s, 3:4], in_=st[:, s, 2:3]), reads=[r_st[s]], writes=[r_st[s]])
                fw.op(fw.dve, lambda e: e.scalar_tensor_tensor(out=yt[:, s, :], in0=yt[:, s, :], scalar=st[:, s, 3:4], in1=G[:, row, :],
                                                                op0=ALU.mult, op1=ALU.mult), reads=[r_st[s], r_G, r_y[s]], writes=[r_y[s]])
                fw.op(fw.dve, lambda e: e.tensor_tensor(out=xt[:, s, :], in0=xt[:, s, :], in1=yt[:, s, :], op=ALU.add),
                      reads=[r_y[s], r_x[s]], writes=[r_x[s]])
                if xdst == "out":
                    fw.dma(fw.sp, f"rn_xo{s}", [(C.out[(t - 2) * 128:(t - 1) * 128, :], xt[:, s, :])], reads=[r_x[s]])
                else:
                    fw.dma(fw.sp, f"rn_xo{s}", [(xdst[rows, :], xt[:, s, :])], reads=[r_x[s]], writes=[xdst_res[t]])
            if not do_norm:
                continue
            fw.op(fw.act, lambda e: e.activation(out=junk[:], in_=xt[:, s, :], func=AF.Square, accum_out=st[:, s, 4:5]),
                  reads=[r_x[s]], writes=[r_junk, r_st[s]])
            fw.op(fw.dve, lambda e: e.tensor_scalar(out=st[:, s, 5:6], in0=st[:, s, 4:5], scalar1=1.0 / D, scalar2=EPS, op0=ALU.mult, op1=ALU.add),
                  reads=[r_st[s]], writes=[r_st[s]])
            fw.op(fw.act, lambda e: e.activation(out=st[:, s, 6:7], in_=st[:, s, 5:6], func=AF.Sqrt), reads=[r_st[s]], writes=[r_st[s]])
            fw.op(fw.dve, lambda e: e.reciprocal(out=st[:, s, 7:8], in_=st[:, s, 6:7]), reads=[r_st[s]], writes=[r_st[s]])
            fw.op(fw.dve, lambda e: e.tensor_scalar(out=xs[:, s, :], in0=xt[:, s, :], scalar1=st[:, s, 7:8], scalar2=None, op0=ALU.mult),
                  reads=[r_st[s], r_x[s]], writes=[r_xs[s]])
            for g8 in range(4):
                b = g8 % 2
                fw.mm(r_p[b], [(pp[b][:, j * 128:(j + 1) * 128], xs[:, s, (g8 * 8 + j) * 128:(g8 * 8 + j + 1) * 128], C.ident[:], True, True)
                               for j in range(8)], reads=[r_xs[s], C.ident_res], transpose=True)
                for j in range(8):
                    kc = g8 * 8 + j
                    E = fw.dve if b == 0 else fw.act
                    if E is fw.dve:
                        fw.op(E, lambda e: e.tensor_scalar(out=hT[:, s, kc, :], in0=pp[b][:, j * 128:(j + 1) * 128],
                                                           scalar1=AB[:, row, 0, kc:kc + 1], scalar2=AB[:, row, 1, kc:kc + 1],
                                                           op0=ALU.mult, op1=ALU.add), reads=[r_p[b], r_AB], writes=[r_hT[s]])
                    else:
                        fw.op(E, lambda e: e.activation(out=hT[:, s, kc, :], in_=pp[b][:, j * 128:(j + 1) * 128], func=AF.Identity,
                                                        scale=AB[:, row, 0, kc:kc + 1], bias=AB[:, row, 1, kc:kc + 1]),
                              reads=[r_p[b], r_AB], writes=[r_hT[s]])
            fw.dma(fw.sp, f"rn_ho{s}", [(hdst[t], hT[:, s])], reads=[r_hT[s]], writes=[hdst_res[t]])


def phase_inproj(C):
    nc, fw = C.nc, C.fw
    li = C.li
    wsrc = C.wb["w_in"]
    blocks = [list(range(0, 9)), list(range(9, 18))]
    FM = {0: (C.aqT_s, 0), 1: (C.aqT_s, 4), 2: (C.aqT_s, 8), 3: (C.aqT_s, 12), 4: (C.akT_s, 0),
          6: (C.rqT_s, 0), 7: (C.rqT_s, 4), 8: (C.rkT_s, 0), 9: (C.rkT_s, 4)}
    TM = {5: (C.av_s, 0, AF.Copy), 10: (C.rv_s, 0, AF.Copy), 11: (C.rv_s, 512, AF.Copy), 12: (C.rv_s, 1024, AF.Copy),
          13: (C.rv_s, 1536, AF.Copy), 14: (C.sg_s, 0, AF.Silu), 15: (C.sg_s, 512, AF.Silu), 16: (C.sg_s, 1024, AF.Silu),
          17: (C.sg_s, 1536, AF.Silu)}
    with ExitStack() as _es:
        hb = _es.enter_context(nc.sbuf_tensor(U("ip_h"), [128, 9, KC, 128], BF16))
        wt = _es.enter_context(nc.sbuf_tensor(U("ip_w"), [128, 2, KC, 512], BF16))
        cos = _es.enter_context(nc.sbuf_tensor(U("ip_cos"), [128, TOK], F32))
        sin = _es.enter_context(nc.sbuf_tensor(U("ip_sin"), [128, TOK], F32))
        t1 = _es.enter_context(nc.sbuf_tensor(U("ip_t1"), [128, 2, 384], F32))
        t2 = _es.enter_context(nc.sbuf_tensor(U("ip_t2"), [128, 2, 384], F32))
        stg = _es.enter_context(nc.sbuf_tensor(U("ip_st"), [128, 4, 512], BF16))
        p0 = _es.enter_context(nc.psum_tensor(U("ip_p0"), [128, 512], F32))
        p1 = _es.enter_context(nc.psum_tensor(U("ip_p1"), [128, 512], F32))
        p2 = _es.enter_context(nc.psum_tensor(U("ip_p2"), [128, 512], F32))
        p3 = _es.enter_context(nc.psum_tensor(U("ip_p3"), [128, 512], F32))
        pp = [p0, p1, p2, p3]
        r_p = [Res() for _ in range(4)]
        r_h = Res()
        r_w = [Res(), Res()]
        r_cs = Res(ro=True)
        r_t1, r_t2 = [Res(), Res()], [Res(), Res()]
        r_stg = [Res() for _ in range(4)]
        fw.dma(fw.sp, "ip_cs", [(cos[:], C.k_cosT), (sin[:], C.k_sinT)], writes=[r_cs])
        wn = 0
        un = 0

        def load_w(g, slot):
            fw.dma(fw.sp, f"ip_w{slot}", [(wt[:, slot], wsrc[li][:, g * 512:(g + 1) * 512].rearrange("(k p) n -> p k n", p=128))],
                   reads=C.wres["w_in"][li], writes=[r_w[slot]])

        for blk in blocks:
            fw.dma(fw.sp, "ip_h", [(hb[:, i], C.hT_s[t]) for i, t in enumerate(blk)], reads=[C.hT_res[t] for t in blk], writes=[r_h])
            load_w(0, wn % 2)
            for g in range(18):
                slot = wn % 2
                wn += 1
                if g + 1 < 18:
                    load_w(g + 1, wn % 2)
                if g in FM:
                    dst, h0 = FM[g]
                    for j in range(4):
                        for sub in range(3):
                            tok0 = (blk[0] + 3 * sub) * 128
                            b = un % 4
                            u2 = un % 2
                            un += 1
                            fw.mm(r_p[b], [(pp[b][:, 0:384], wt[:, slot, kc, j * 128:(j + 1) * 128], hb[:, 3 * sub:3 * sub + 3, kperm("w_in", kc), :], kc == 0, kc == KC - 1)
                                           for kc in range(KC)], reads=[r_w[slot], r_h])
                            ps = pp[b]
                            cs_ = cos[:, tok0:tok0 + 384]
                            fw.op(fw.dve, lambda e: e.tensor_tensor(out=t1[:, u2, :], in0=ps[:, 0:384], in1=cs_, op=ALU.mult),
                                  reads=[r_p[b], r_cs], writes=[r_t1[u2]])
                            for q in range(4):
                                src = q * 32 + 32 if q % 2 == 0 else q * 32 - 32
                                fw.op(fw.dve, lambda e: e.tensor_tensor(out=t2[q * 32:q * 32 + 32, u2, :], in0=ps[src:src + 32, 0:384],
                                                                        in1=sin[q * 32:q * 32 + 32, tok0:tok0 + 384], op=ALU.mult),
                                      reads=[r_p[b], r_cs], writes=[r_t2[u2]])
                            fw.op(fw.dve, lambda e: e.tensor_tensor(out=stg[:, b, 0:384], in0=t1[:, u2, :], in1=t2[:, u2, :], op=ALU.add),
                                  reads=[r_t1[u2], r_t2[u2]], writes=[r_stg[b]])
                            fw.dma(fw.sp, f"ip_o{b}", [(dst[h0 + j, :, tok0:tok0 + 384], stg[:, b, 0:384])], reads=[r_stg[b]], writes=[C.proj_res])
                else:
                    dst, c0, func = TM[g]
                    for i, t in enumerate(blk):
                        b = un % 4
                        un += 1
                        fw.mm(r_p[b], [(pp[b][:], hb[:, i, kperm("w_in", kc), :], wt[:, slot, kc, :], kc == 0, kc == KC - 1) for kc in range(KC)],
                              reads=[r_w[slot], r_h])
                        ps = pp[b]
                        fw.op(fw.act, lambda e: e.activation(out=stg[:, b, :], in_=ps[:], func=func), reads=[r_p[b]], writes=[r_stg[b]])
                        fw.dma(fw.sp, f"ip_o{b}", [(dst[t * 128:(t + 1) * 128, c0:c0 + 512], stg[:, b, :])], reads=[r_stg[b]], writes=[C.proj_res])


def phase_ret(C):
    nc, fw = C.nc, C.fw
    li = C.li
    out_tiles = set(C.tiles)
    with ExitStack() as _es:
        tab = _es.enter_context(nc.sbuf_tensor(U("rt_tab"), [128, 6, 128], F32))
        pcol = _es.enter_context(nc.sbuf_tensor(U("rt_pcol"), [128, 3], F32))
        lg = _es.enter_context(nc.sbuf_tensor(U("rt_lg"), [128, 16], F32))
        DT = _es.enter_context(nc.sbuf_tensor(U("rt_DT"), [128, 8, 128], F32))
        dtmp = _es.enter_context(nc.sbuf_tensor(U("rt_tmp"), [128, 2, 128], F32))
        qrow = _es.enter_context(nc.sbuf_tensor(U("rt_qrow"), [128, 8, 2, 128], F32))
        kd = _es.enter_context(nc.sbuf_tensor(U("rt_kd"), [128, 8, 4], F32))
        gn = _es.enter_context(nc.sbuf_tensor(U("rt_gn"), [128, 8, 256], F32))
        qT = _es.enter_context(nc.sbuf_tensor(U("rt_q"), [128, 2, TOK], BF16))
        kT = _es.enter_context(nc.sbuf_tensor(U("rt_k"), [128, 2, TOK], BF16))
        vv = _es.enter_context(nc.sbuf_tensor(U("rt_v"), [128, 2, NT, 256], BF16))
        gg = _es.enter_context(nc.sbuf_tensor(U("rt_g"), [128, 2, NT, 256], BF16))
        kf = _es.enter_context(nc.sbuf_tensor(U("rt_kf"), [128, NT, 128], BF16))
        kb = _es.enter_context(nc.sbuf_tensor(U("rt_kb"), [128, 128], BF16))
        S = _es.enter_context(nc.sbuf_tensor(U("rt_S"), [128, 2, 256], F32))
        Sfb = _es.enter_context(nc.sbuf_tensor(U("rt_Sfb"), [128, 256], BF16))
        Sbb = _es.enter_context(nc.sbuf_tensor(U("rt_Sbb"), [128, NT, 256], BF16))
        A = _es.enter_context(nc.sbuf_tensor(U("rt_A"), [128, 2, 128], BF16))
        qd = _es.enter_context(nc.sbuf_tensor(U("rt_qd"), [128, 2, 2, 128], BF16))
        st = _es.enter_context(nc.sbuf_tensor(U("rt_st"), [128, 2, 12], F32))
        yn = _es.enter_context(nc.sbuf_tensor(U("rt_yn"), [128, 2, 256], F32))
        yb = _es.enter_context(nc.sbuf_tensor(U("rt_yb"), [128, 2, 256], BF16))
        ro = _es.enter_context(nc.sbuf_tensor(U("rt_o"), [128, 2, 2, NT, 128], BF16))
        pk = _es.enter_context(nc.psum_tensor(U("rt_pk"), [128, 1024], BF16))
        pt = _es.enter_context(nc.psum_tensor(U("rt_pt"), [128, 1024], BF16))
        psc = _es.enter_context(nc.psum_tensor(U("rt_ps"), [128, 512], F32))
        py0 = _es.enter_context(nc.psum_tensor(U("rt_py0"), [128, 512], F32))
        py1 = _es.enter_context(nc.psum_tensor(U("rt_py1"), [128, 512], F32))
        pkv0 = _es.enter_context(nc.psum_tensor(U("rt_pkv0"), [128, 512], F32))
        pkv1 = _es.enter_context(nc.psum_tensor(U("rt_pkv1"), [128, 512], F32))
        r_tab, r_lg, r_DT, r_dtmp, r_qrow, r_kd, r_gn = Res(ro=True), Res(), Res(), Res(), Res(), Res(), Res()
        r_in = [Res(), Res()]
        r_kf, r_kb, r_S, r_Sfb, r_Sbb = Res(), Res(), Res(), Res(), Res()
        r_A, r_qd, r_st, r_yn, r_yb = [Res(), Res()], [Res(), Res()], [Res(), Res()], [Res(), Res()], [Res(), Res()]
        r_ro = [Res(), Res()]
        r_pk, r_pt, r_psc = Res(), Res(), Res()
        r_py, r_pkv = [Res(), Res()], [Res(), Res()]
        py, pkv = [py0, py1], [pkv0, pkv1]
        fw.dma(fw.sp, "rt_c", [(tab[:], C.k_rett), (pcol[:], C.k_pcol)], writes=[r_tab])
        fw.dma(fw.sp, "rt_lg", [(lg[:], C.ret_log_decay[li].partition_broadcast(128))], writes=[r_lg])
        fw.dma(fw.sp, "rt_gn", [(gn[:, h, :], C.ret_gn_g[li, h].partition_broadcast(128)) for h in range(8)], writes=[r_gn])
        for h in range(8):
            lf = lg[:, h:h + 1]
            lb = lg[:, 8 + h:8 + h + 1]
            fw.op(fw.act, lambda e: e.activation(out=dtmp[:, 0, :], in_=tab[:, 0, :], func=AF.Exp, scale=lf), reads=[r_tab, r_lg], writes=[r_dtmp])
            fw.op(fw.act, lambda e: e.activation(out=dtmp[:, 1, :], in_=tab[:, 2, :], func=AF.Exp, scale=lb), reads=[r_tab, r_lg], writes=[r_dtmp])
            fw.op(fw.dve, lambda e: e.tensor_tensor(out=dtmp[:, 0, :], in0=dtmp[:, 0, :], in1=tab[:, 1, :], op=ALU.mult), reads=[r_tab], writes=[r_dtmp])
            fw.op(fw.dve, lambda e: e.tensor_tensor(out=dtmp[:, 1, :], in0=dtmp[:, 1, :], in1=tab[:, 3, :], op=ALU.mult), reads=[r_tab], writes=[r_dtmp])
            fw.op(fw.dve, lambda e: e.tensor_tensor(out=DT[:, h, :], in0=dtmp[:, 0, :], in1=dtmp[:, 1, :], op=ALU.add), reads=[r_dtmp], writes=[r_DT])
            fw.op(fw.act, lambda e: e.activation(out=qrow[:, h, 0, :], in_=tab[:, 4, :], func=AF.Exp, scale=lf), reads=[r_tab, r_lg], writes=[r_qrow])
            fw.op(fw.act, lambda e: e.activation(out=qrow[:, h, 1, :], in_=tab[:, 5, :], func=AF.Exp, scale=lb), reads=[r_tab, r_lg], writes=[r_qrow])
            fw.op(fw.dve, lambda e: e.tensor_scalar(out=qrow[:, h], in0=qrow[:, h], scalar1=SCALE, scalar2=None, op0=ALU.mult), reads=[], writes=[r_qrow])
            fw.op(fw.act, lambda e: e.activation(out=kd[:, h, 0:1], in_=pcol[:, 0:1], func=AF.Exp, scale=lf), reads=[r_tab, r_lg], writes=[r_kd])
            fw.op(fw.act, lambda e: e.activation(out=kd[:, h, 1:2], in_=pcol[:, 1:2], func=AF.Exp, scale=lb), reads=[r_tab, r_lg], writes=[r_kd])
            fw.op(fw.act, lambda e: e.activation(out=kd[:, h, 2:3], in_=pcol[:, 2:3], func=AF.Exp, scale=lf), reads=[r_tab, r_lg], writes=[r_kd])
            fw.op(fw.act, lambda e: e.activation(out=kd[:, h, 3:4], in_=pcol[:, 2:3], func=AF.Exp, scale=lb), reads=[r_tab, r_lg], writes=[r_kd])

        def load_head(h, s):
            fw.dma(fw.sp, f"rt_in{s}", [
                (qT[:, s, :], C.rqT_s[h]), (kT[:, s, :], C.rkT_s[h]),
                (vv[:, s], C.rv_s[:, h * 256:(h + 1) * 256].rearrange("(t p) e -> p t e", p=128)),
                (gg[:, s], C.sg_s[:, h * 256:(h + 1) * 256].rearrange("(t p) e -> p t e", p=128))],
                reads=[C.proj_res], writes=[r_in[s]])

        border = [1, 0] + list(range(17, 1, -1))
        load_head(0, 0)
        un = 0
        for h in range(8):
            s = h % 2
            if h + 1 < 8:
                load_head(h + 1, (h + 1) % 2)
            fw.op(fw.dve, lambda e: e.memset(S[:], 0.0), writes=[r_S])
            for t in border:
                tk = slice(t * 128, (t + 1) * 128)
                fw.op(fw.act, lambda e: e.activation(out=Sbb[:, t, :], in_=S[:, 1, :], func=AF.Copy), reads=[r_S], writes=[r_Sbb])
                fw.mm(r_pk, [(pk[:, 0:128], kT[:, s, tk], C.ident[:], True, True)], reads=[r_in[s], C.ident_res], transpose=True)
                fw.op(fw.dve, lambda e: e.tensor_scalar(out=kf[:, t, :], in0=pk[:, 0:128], scalar1=kd[:, h, 0:1], scalar2=None, op0=ALU.mult),
                      reads=[r_pk, r_kd], writes=[r_kf])
                fw.op(fw.dve, lambda e: e.tensor_scalar(out=kb[:], in0=pk[:, 0:128], scalar1=kd[:, h, 1:2], scalar2=None, op0=ALU.mult),
                      reads=[r_pk, r_kd], writes=[r_kb])
                b = un % 2
                un += 1
                fw.mm(r_pkv[b], [(pkv[b][:, 0:256], kb[:], vv[:, s, t, :], True, True)], reads=[r_kb, r_in[s]])
                fw.op(fw.dve, lambda e: e.scalar_tensor_tensor(out=S[:, 1, :], in0=S[:, 1, :], scalar=kd[:, h, 3:4], in1=pkv[b][:, 0:256],
                                                                op0=ALU.mult, op1=ALU.add), reads=[r_pkv[b], r_kd], writes=[r_S])
            for t in range(NT):
                tk = slice(t * 128, (t + 1) * 128)
                u = t % 2
                need_out = t in out_tiles
                fw.op(fw.act, lambda e: e.activation(out=Sfb[:], in_=S[:, 0, :], func=AF.Copy), reads=[r_S], writes=[r_Sfb])
                if need_out:
                    fw.mm(r_psc, [(psc[:, 0:128], kT[:, s, tk], qT[:, s, tk], True, True)], reads=[r_in[s]])
                    fw.op(fw.dve, lambda e: e.tensor_tensor(out=A[:, u, :], in0=psc[:, 0:128], in1=DT[:, h, :], op=ALU.mult),
                          reads=[r_psc, r_DT], writes=[r_A[u]])
                    fw.op(fw.dve, lambda e: e.tensor_tensor(out=qd[:, u, 0, :], in0=qT[:, s, tk], in1=qrow[:, h, 0, :], op=ALU.mult),
                          reads=[r_in[s], r_qrow], writes=[r_qd[u]])
                    fw.op(fw.dve, lambda e: e.tensor_tensor(out=qd[:, u, 1, :], in0=qT[:, s, tk], in1=qrow[:, h, 1, :], op=ALU.mult),
                          reads=[r_in[s], r_qrow], writes=[r_qd[u]])
                    fw.mm(r_py[u], [(py[u][:, 0:256], A[:, u, :], vv[:, s, t, :], True, False),
                                    (py[u][:, 0:256], qd[:, u, 0, :], Sfb[:], False, False),
                                    (py[u][:, 0:256], qd[:, u, 1, :], Sbb[:, t, :], False, True)],
                          reads=[r_A[u], r_qd[u], r_in[s], r_Sfb, r_Sbb])
                b = un % 2
                un += 1
                fw.mm(r_pkv[b], [(pkv[b][:, 0:256], kf[:, t, :], vv[:, s, t, :], True, True)], reads=[r_kf, r_in[s]])
                fw.op(fw.dve, lambda e: e.scalar_tensor_tensor(out=S[:, 0, :], in0=S[:, 0, :], scalar=kd[:, h, 2:3], in1=pkv[b][:, 0:256],
                                                                op0=ALU.mult, op1=ALU.add), reads=[r_pkv[b], r_kd, r_Sfb], writes=[r_S])
                if not need_out:
                    continue
                fw.op(fw.dve, lambda e: e.bn_stats(out=st[:, u, 0:6], in_=py[u][:, 0:256]), reads=[r_py[u]], writes=[r_st[u]])
                fw.op(fw.dve, lambda e: e.bn_aggr(out=st[:, u, 6:8], in_=st[:, u, 0:6]), reads=[r_st[u]], writes=[r_st[u]])
                fw.op(fw.dve, lambda e: e.tensor_scalar(out=st[:, u, 8:9], in0=st[:, u, 7:8], scalar1=EPS, scalar2=None, op0=ALU.add),
                      reads=[r_st[u]], writes=[r_st[u]])
                fw.op(fw.act, lambda e: e.activation(out=st[:, u, 9:10], in_=st[:, u, 8:9], func=AF.Sqrt), reads=[r_st[u]], writes=[r_st[u]])
                fw.op(fw.dve, lambda e: e.reciprocal(out=st[:, u, 10:11], in_=st[:, u, 9:10]), reads=[r_st[u]], writes=[r_st[u]])
                fw.op(fw.dve, lambda e: e.tensor_scalar(out=yn[:, u, :], in0=py[u][:, 0:256], scalar1=st[:, u, 6:7], scalar2=st[:, u, 10:11],
                                                        op0=ALU.subtract, op1=ALU.mult), reads=[r_py[u], r_st[u]], writes=[r_yn[u]])
                fw.op(fw.dve, lambda e: e.tensor_tensor(out=yn[:, u, :], in0=yn[:, u, :], in1=gn[:, h, :], op=ALU.mult), reads=[r_gn], writes=[r_yn[u]])
                fw.op(fw.dve, lambda e: e.tensor_tensor(out=yb[:, u, :], in0=yn[:, u, :], in1=gg[:, s, t, :], op=ALU.mult),
                      reads=[r_yn[u], r_in[s]], writes=[r_yb[u]])
                fw.mm(r_pt, [(pt[:, j * 128:(j + 1) * 128], yb[:, u, j * 128:(j + 1) * 128], C.ident[:], True, True) for j in range(2)],
                      reads=[r_yb[u], C.ident_res], transpose=True)
                fw.op(fw.act, lambda e: e.activation(out=ro[:, s, :, t, :], in_=pt[:, 0:256].rearrange("p (j c) -> p j c", j=2), func=AF.Copy),
                      reads=[r_pt], writes=[r_ro[s]])
            tl = sorted(out_tiles)
            t0, t1_ = tl[0], tl[-1] + 1
            fw.dma(fw.sp, f"rt_o{s}", [(C.mixT_s[t0:t1_, :, 16 + 2 * h + j, :].rearrange("t p c -> p t c"), ro[:, s, j, t0:t1_, :]) for j in range(2)],
                   reads=[r_ro[s]], writes=[C.mix_res[t] for t in tl])


def phase_att(C):
    nc, fw = C.nc, C.fw
    li = C.li
    with ExitStack() as _es:
        mprev = _es.enter_context(nc.sbuf_tensor(U("at_mp"), [128, 512], BF16))
        mnext = _es.enter_context(nc.sbuf_tensor(U("at_mn"), [128, 512], BF16))
        ones = _es.enter_context(nc.sbuf_tensor(U("at_ones"), [128, 128], BF16))
        f1 = _es.enter_context(nc.sbuf_tensor(U("at_f1"), [128, 128], F32))
        sk = _es.enter_context(nc.sbuf_tensor(U("at_sk"), [128, 16], F32))
        es = _es.enter_context(nc.sbuf_tensor(U("at_es"), [128, 16, 128], F32))
        kT = _es.enter_context(nc.sbuf_tensor(U("at_k"), [128, 2, TOK], BF16))
        qT = _es.enter_context(nc.sbuf_tensor(U("at_q"), [128, 2, 4, TOK], BF16))
        vv = _es.enter_context(nc.sbuf_tensor(U("at_v"), [128, 2, NT, 128], BF16))
        pT = _es.enter_context(nc.sbuf_tensor(U("at_p"), [128, 3, 512], BF16))
        dn = _es.enter_context(nc.sbuf_tensor(U("at_dn"), [128, 2, 512], F32))
        ao = _es.enter_context(nc.sbuf_tensor(U("at_o"), [128, 2, 4, NT, 128], BF16))
        s0 = _es.enter_context(nc.psum_tensor(U("at_s0"), [128, 512], F32))
        s1 = _es.enter_context(nc.psum_tensor(U("at_s1"), [128, 512], F32))
        s2 = _es.enter_context(nc.psum_tensor(U("at_s2"), [128, 512], F32))
        O0 = _es.enter_context(nc.psum_tensor(U("at_O0"), [128, 512], F32))
        O1 = _es.enter_context(nc.psum_tensor(U("at_O1"), [128, 512], F32))
        D0 = _es.enter_context(nc.psum_tensor(U("at_D0"), [128, 512], F32))
        D1 = _es.enter_context(nc.psum_tensor(U("at_D1"), [128, 512], F32))
        r_c, r_sk, r_es = Res(ro=True), Res(), Res()
        r_in = [Res(), Res()]
        r_pT = [Res() for _ in range(3)]
        r_ps = [Res() for _ in range(3)]
        r_O, r_D, r_dn, r_ao = [Res(), Res()], [Res(), Res()], [Res(), Res()], [Res(), Res()]
        pss, OO, DD = [s0, s1, s2], [O0, O1], [D0, D1]
        fw.dma(fw.sp, "at_c", [(mprev[:], C.k_mprev), (mnext[:], C.k_mnext), (ones[:], C.k_onesb)], writes=[r_c])
        fw.dma(fw.sp, "at_sk", [(sk[:], C.attn_sink[li].partition_broadcast(128))], writes=[r_sk])
        fw.op(fw.act, lambda e: e.activation(out=sk[:], in_=sk[:], func=AF.Exp), reads=[], writes=[r_sk])
        fw.op(fw.dve, lambda e: e.memset(f1[:], 1.0), writes=[r_es])
        for hh in range(16):
            fw.op(fw.dve, lambda e: e.tensor_scalar(out=es[:, hh, :], in0=f1[:], scalar1=sk[:, hh:hh + 1], scalar2=None, op0=ALU.mult),
                  reads=[r_sk], writes=[r_es])

        def load_g(g, s):
            fw.dma(fw.sp, f"at_in{s}", [
                (kT[:, s, :], C.akT_s[g]),
                (qT[:, s], C.aqT_s[4 * g:4 * g + 4].rearrange("h p t -> p h t")),
                (vv[:, s], C.av_s[:, g * 128:(g + 1) * 128].rearrange("(t p) d -> p t d", p=128))],
                reads=[C.proj_res], writes=[r_in[s]])

        load_g(0, 0)
        un = 0
        qn = 0
        tl = sorted(C.tiles)
        for g in range(4):
            s = g % 2
            if g + 1 < 4:
                load_g(g + 1, (g + 1) % 2)
            for t in tl:
                if t < 2:
                    keys = [(0, None), (1, None)]
                else:
                    keys = []
                    if t - 1 >= 2:
                        keys.append((t - 1, mprev))
                    keys.append((t, None))
                    if t + 1 < NT:
                        keys.append((t + 1, mnext))
                    keys += [(0, None), (1, None)]
                o = qn % 2
                qn += 1
                tq = slice(t * 128, (t + 1) * 128)
                for ki, (j, mask) in enumerate(keys):
                    b = un % 3
                    un += 1
                    fw.mm(r_ps[b], [(pss[b][:], kT[:, s, j * 128:(j + 1) * 128], qT[:, s, :, tq], True, True)], reads=[r_in[s]])
                    psb = pss[b]
                    fw.op(fw.act, lambda e: e.activation(out=pT[:, b, :], in_=psb[:], func=AF.Exp, scale=SCALE), reads=[r_ps[b]], writes=[r_pT[b]])
                    if mask is not None:
                        fw.op(fw.dve, lambda e: e.tensor_tensor(out=pT[:, b, :], in0=pT[:, b, :], in1=mask[:], op=ALU.mult), reads=[r_c], writes=[r_pT[b]])
                    first, lastk = ki == 0, ki == len(keys) - 1
                    fw.mm(r_O[o], [(OO[o][:], vv[:, s, j, :], pT[:, b, :], first, lastk)], reads=[r_pT[b], r_in[s]])
                    fw.mm(r_D[o], [(DD[o][:], ones[:], pT[:, b, :], first, lastk)], reads=[r_pT[b], r_c])
                fw.op(fw.dve, lambda e: e.tensor_tensor(out=dn[:, o, :], in0=DD[o][:], in1=es[:, 4 * g:4 * g + 4, :].rearrange("p h c -> p (h c)"), op=ALU.add),
                      reads=[r_D[o], r_es], writes=[r_dn[o]])
                fw.op(fw.dve, lambda e: e.reciprocal(out=dn[:, o, :], in_=dn[:, o, :]), reads=[], writes=[r_dn[o]])
                fw.op(fw.dve, lambda e: e.tensor_tensor(out=ao[:, s, :, t, :], in0=OO[o][:].rearrange("p (h c) -> p h c", h=4),
                                                        in1=dn[:, o, :].rearrange("p (h c) -> p h c", h=4), op=ALU.mult),
                      reads=[r_O[o], r_dn[o]], writes=[r_ao[s]])
            t0, t1_ = tl[0], tl[-1] + 1
            fw.dma(fw.sp, f"at_o{s}", [(C.mixT_s[t0:t1_, :, 4 * g + hq, :].rearrange("t p c -> p t c"), ao[:, s, hq, t0:t1_, :]) for hq in range(4)],
                   reads=[r_ao[s]], writes=[C.mix_res[t] for t in tl])


def tile_blocks(tiles, n):
    return [tiles[i:i + n] for i in range(0, len(tiles), n)]


def phase_proj_out(C, wname, src_s, src_res):
    nc, fw = C.nc, C.fw
    li = C.li
    wsrc = C.wb[wname]
    blocks = tile_blocks(C.tiles, 9)
    with ExitStack() as _es:
        ab = _es.enter_context(nc.sbuf_tensor(U("po_a"), [128, 9, KC, 128], BF16))
        wt = _es.enter_context(nc.sbuf_tensor(U("po_w"), [128, 2, KC, 512], BF16))
        stg = _es.enter_context(nc.sbuf_tensor(U("po_st"), [128, 4, 512], F32))
        junk = _es.enter_context(nc.sbuf_tensor(U("po_junk"), [128, 512], BF16))
        p0 = _es.enter_context(nc.psum_tensor(U("po_p0"), [128, 512], F32))
        p1 = _es.enter_context(nc.psum_tensor(U("po_p1"), [128, 512], F32))
        p2 = _es.enter_context(nc.psum_tensor(U("po_p2"), [128, 512], F32))
        p3 = _es.enter_context(nc.psum_tensor(U("po_p3"), [128, 512], F32))
        pp = [p0, p1, p2, p3]
        r_p = [Res() for _ in range(4)]
        r_a, r_junk = Res(), Res()
        r_w = [Res(), Res()]
        r_stg = [Res() for _ in range(4)]
        wn = 0
        un = 0

        def load_w(cb, slot):
            fw.dma(fw.sp, f"po_w{slot}", [(wt[:, slot], wsrc[li][:, cb * 512:(cb + 1) * 512].rearrange("(k p) n -> p k n", p=128))],
                   reads=C.wres[wname][li], writes=[r_w[slot]])

        for blk in blocks:
            fw.dma(fw.sp, "po_a", [(ab[:, i], src_s[t]) for i, t in enumerate(blk)], reads=[src_res[t] for t in blk], writes=[r_a])
            load_w(0, wn % 2)
            for cb in range(8):
                slot = wn % 2
                wn += 1
                if cb + 1 < 8:
                    load_w(cb + 1, wn % 2)
                for i, t in enumerate(blk):
                    b = un % 4
                    un += 1
                    fw.mm(r_p[b], [(pp[b][:], ab[:, i, kperm(wname, kc), :], wt[:, slot, kc, :], kc == 0, kc == KC - 1) for kc in range(KC)],
                          reads=[r_w[slot], r_a])
                    ps = pp[b]
                    fw.op(fw.dve, lambda e: e.tensor_copy(out=stg[:, b, :], in_=ps[:]), reads=[r_p[b]], writes=[r_stg[b]])
                    fw.op(fw.act, lambda e: e.activation(out=junk[:], in_=stg[:, b, :], func=AF.Square, accum_out=C.ssq[:, t, cb:cb + 1]),
                          reads=[r_stg[b]], writes=[r_junk, C.ssq_res])
                    fw.dma(fw.sp, f"po_o{b}", [(C.y_s[t * 128:(t + 1) * 128, cb * 512:(cb + 1) * 512], stg[:, b, :])],
                           reads=[r_stg[b]], writes=[C.y_res[t]])


def phase_mlp(C):
    nc, fw = C.nc, C.fw
    li = C.li
    wu, wd = C.wb["w_up"], C.wb["w_down"]
    blocks = tile_blocks(C.tiles, 3)
    if len(blocks[-1]) == 1:
        tail = blocks[-2] + blocks[-1]
        blocks = blocks[:-2] + [tail[:2], tail[2:]]
    with ExitStack() as _es:
        hb = _es.enter_context(nc.sbuf_tensor(U("ml_h"), [128, 3, KC, 128], BF16))
        uT = _es.enter_context(nc.sbuf_tensor(U("ml_u"), [128, 128, 384], BF16))
        wut = _es.enter_context(nc.sbuf_tensor(U("ml_wu"), [128, 2, KC, 256], BF16))
        wdt = _es.enter_context(nc.sbuf_tensor(U("ml_wd"), [128, 2, 16, 512], BF16))
        rl = _es.enter_context(nc.sbuf_tensor(U("ml_r"), [128, 2, 384], BF16))
        stg = _es.enter_context(nc.sbuf_tensor(U("ml_st"), [128, 2, 512], F32))
        junk = _es.enter_context(nc.sbuf_tensor(U("ml_junk"), [128, 512], BF16))
        p0 = _es.enter_context(nc.psum_tensor(U("ml_p0"), [128, 512], F32))
        p1 = _es.enter_context(nc.psum_tensor(U("ml_p1"), [128, 512], F32))
        a0 = _es.enter_context(nc.psum_tensor(U("ml_a0"), [128, 512], F32))
        a1 = _es.enter_context(nc.psum_tensor(U("ml_a1"), [128, 512], F32))
        a2 = _es.enter_context(nc.psum_tensor(U("ml_a2"), [128, 512], F32))
        a3 = _es.enter_context(nc.psum_tensor(U("ml_a3"), [128, 512], F32))
        a4 = _es.enter_context(nc.psum_tensor(U("ml_a4"), [128, 512], F32))
        a5 = _es.enter_context(nc.psum_tensor(U("ml_a5"), [128, 512], F32))
        pu = [p0, p1]
        acc = [[a0, a1, a2], [a3, a4, a5]]
        r_pu = [Res(), Res()]
        r_acc = [[Res() for _ in range(3)] for _ in range(2)]
        r_h, r_u, r_junk = Res(), Res(), Res()
        r_wu, r_wd, r_rl, r_stg = [Res(), Res()], [Res(), Res()], [Res(), Res()], [Res(), Res()]
        nwu = 0
        nwd = 0
        un = 0
        sn = 0
        cbn = 0

        def load_wu(fg, slot):
            fw.dma(fw.sp, f"ml_wu{slot}", [(wut[:, slot], wu[li][:, fg * 256:(fg + 1) * 256].rearrange("(k p) n -> p k n", p=128))],
                   reads=C.wres["w_up"][li], writes=[r_wu[slot]])

        def load_wd(cb, fgrp, slot):
            fw.dma(fw.sp, f"ml_wd{slot}", [(wdt[:, slot], wd[li][fgrp * 2048:(fgrp + 1) * 2048, cb * 512:(cb + 1) * 512].rearrange("(k p) n -> p k n", p=128))],
                   reads=C.wres["w_down"][li], writes=[r_wd[slot]])

        for blk in blocks:
            nt_ = len(blk)
            N = nt_ * 128
            fw.dma(fw.sp, "ml_h", [(hb[:, i], C.h2T_s[t]) for i, t in enumerate(blk)], reads=[C.h2T_res[t] for t in blk], writes=[r_h])
            load_wu(0, nwu % 2)
            for fg in range(64):
                slot = nwu % 2
                nwu += 1
                if fg + 1 < 64:
                    load_wu(fg + 1, nwu % 2)
                for j in range(2):
                    b = un % 2
                    un += 1
                    fw.mm(r_pu[b], [(pu[b][:, 0:N], wut[:, slot, kc, j * 128:(j + 1) * 128], hb[:, 0:nt_, kperm("w_up", kc), :], kc == 0, kc == KC - 1) for kc in range(KC)],
                          reads=[r_wu[slot], r_h])
                    ps = pu[b]
                    fw.op(fw.act, lambda e: e.activation(out=rl[:, b, 0:N], in_=ps[:, 0:N], func=AF.Relu), reads=[r_pu[b]], writes=[r_rl[b]])
                    fw.op(fw.dve, lambda e: e.tensor_tensor(out=uT[:, fg * 2 + j, 0:N], in0=rl[:, b, 0:N], in1=rl[:, b, 0:N], op=ALU.mult),
                          reads=[r_rl[b]], writes=[r_u])
            load_wd(0, 0, nwd % 2)
            for cb in range(8):
                aset = cbn % 2
                cbn += 1
                for fgrp in range(8):
                    slot = nwd % 2
                    nwd += 1
                    nxt = (cb, fgrp + 1) if fgrp + 1 < 8 else ((cb + 1, 0) if cb + 1 < 8 else None)
                    if nxt is not None:
                        load_wd(nxt[0], nxt[1], nwd % 2)
                    for i in range(nt_):
                        fw.mm(r_acc[aset][i], [(acc[aset][i][:], uT[:, kperm("w_down", fgrp * 16 + k), i * 128:(i + 1) * 128], wdt[:, slot, k, :],
                                                fgrp == 0 and k == 0, fgrp == 7 and k == 15) for k in range(16)],
                              reads=[r_wd[slot], r_u])
                for i, t in enumerate(blk):
                    sb = sn % 2
                    sn += 1
                    ps = acc[aset][i]
                    fw.op(fw.dve, lambda e: e.tensor_copy(out=stg[:, sb, :], in_=ps[:]), reads=[r_acc[aset][i]], writes=[r_stg[sb]])
                    fw.op(fw.act, lambda e: e.activation(out=junk[:], in_=stg[:, sb, :], func=AF.Square, accum_out=C.ssq[:, t, cb:cb + 1]),
                          reads=[r_stg[sb]], writes=[r_junk, C.ssq_res])
                    fw.dma(fw.sp, f"ml_o{sb}", [(C.y_s[t * 128:(t + 1) * 128, cb * 512:(cb + 1) * 512], stg[:, sb, :])],
                           reads=[r_stg[sb]], writes=[C.y_res[t]])


_CONSTS = None


def make_in_maps(inputs, cores):
    global _CONSTS
    if _CONSTS is None:
        _CONSTS = _consts()
    K = _CONSTS
    f = np.float32
    A = lambda a: np.ascontiguousarray(a, dtype=f)
    x, c, ctx, c_ctx = inputs["x"], inputs["c"], inputs["ctx"], inputs["c_ctx"]
    norm_g = A(inputs["norm_g"])
    norm_gT = A(norm_g.reshape(DEPTH, 4, KC, 128).transpose(0, 1, 3, 2))
    c9 = np.concatenate([np.asarray(c, dtype=f), np.asarray(c_ctx, dtype=f)[None, :]], 0)
    c9T = A(c9.T.reshape(KC, 128, 9).transpose(1, 0, 2))
    w_ada = np.asarray(inputs["w_ada"]).reshape(DEPTH, D, 6, 8, 512)
    b_ada = np.asarray(inputs["b_ada"]).reshape(DEPTH, 6, 8, 512)
    shared = dict(norm_g=norm_g, norm_gT=norm_gT, attn_sink=A(inputs["attn_sink"]),
                  ret_log_decay=A(np.asarray(inputs["ret_log_decay"]).reshape(DEPTH, 16)),
                  ret_gn_g=A(inputs["ret_gn_g"]), c9T=c9T, **K)
    maps = []
    for b in cores:
        sel = np.zeros((9, 2), f)
        sel[b, 0] = 1.0
        sel[8, 1] = 1.0
        m = dict(shared)
        m.update(x=A(x[b]), ctx=A(ctx[b]), sel=sel,
                 w_ada_sh=A(w_ada[:, :, :, b, :].reshape(DEPTH, D, 3072)), b_ada_sh=A(b_ada[:, :, b, :].reshape(DEPTH, 3072)),
                 w_in_sh=A(inputs["w_in"][:, b * 512:(b + 1) * 512, :]), w_out_sh=A(inputs["w_out"][:, b * 512:(b + 1) * 512, :]),
                 w_up_sh=A(inputs["w_up"][:, b * 512:(b + 1) * 512, :]), w_down_sh=A(inputs["w_down"][:, b * 2048:(b + 1) * 2048, :]))
        maps.append(m)
    return maps


_NC = None


def kernel(**inputs):
    global _NC
    if _NC is None:
        _NC = build()
    maps = make_in_maps(inputs, list(range(8)))
    res = run_bass_kernel_spmd(_NC, maps, core_ids=list(range(8)))
    return np.stack([np.asarray(r["out"], dtype=np.float32) for r in res.results], 0)
```

```python
from contextlib import ExitStack
import numpy as np
import ml_dtypes
import concourse.bass as bass
import concourse.mybir as mybir
from concourse.bass_utils import run_bass_kernel_spmd

F32 = mybir.dt.float32
BF16 = mybir.dt.bfloat16
AF = mybir.ActivationFunctionType
ALU = mybir.AluOpType
AX = mybir.AxisListType

NT = 18
TOK = 2304
D = 4096
KC = 32
DFF = 16384
EPS = 1e-6
DEPTH = 2
SCALE = 128.0 ** -0.5
NDUMMY = 0
PRECOUNT = 0


class Res:
    __slots__ = ("w", "r", "ro")

    def __init__(self, ro=False):
        self.w = {}
        self.r = {}
        self.ro = ro


class Eng:
    def __init__(self, fw, eng, name):
        self.eng = eng
        self.name = name
        self.sem = fw.nc.alloc_semaphore("s_" + name)
        fw.sems[self.sem.num] = self.sem
        self.count = 0
        self.known = {}


class FW:
    def __init__(self, nc):
        self.nc = nc
        self.sems = {}
        self.pe = Eng(self, nc.tensor, "pe")
        self.act = Eng(self, nc.scalar, "act")
        self.dve = Eng(self, nc.vector, "dve")
        self.pool = Eng(self, nc.gpsimd, "pool")
        self.sp = Eng(self, nc.sync, "sp")
        self.engs = [self.pe, self.act, self.dve, self.pool, self.sp]
        if PRECOUNT:
            for _ in range(PRECOUNT // 1000):
                self.dve.eng.sem_inc(self.dve.sem, 1000)
            self.dve.count = PRECOUNT
        self.dsem = {}
        self.free_dsems = []
        for i in range(NDUMMY):
            nc.alloc_semaphore(f"dummy{i}")
        self.bar = nc.alloc_semaphore("s_bar")
        self.barcount = 0
        self.ninst = 0

    def _deps(self, reads, writes):
        deps = {}
        for r in reads:
            for s, v in r.w.items():
                if deps.get(s, 0) < v:
                    deps[s] = v
        for w in writes:
            for s, v in w.w.items():
                if deps.get(s, 0) < v:
                    deps[s] = v
            for s, v in w.r.items():
                if deps.get(s, 0) < v:
                    deps[s] = v
        return deps

    def _wait(self, E, deps, skip_own=False):
        for num, val in deps.items():
            if skip_own and num == E.sem.num:
                continue
            if E.known.get(num, 0) >= val:
                continue
            E.eng.wait_ge(self.sems[num], val)
            E.known[num] = val
            self.ninst += 1

    def _finish(self, ev, reads, writes):
        for r in reads:
            if not r.ro and r.r.get(ev[0], 0) < ev[1]:
                r.r[ev[0]] = ev[1]
        for w in writes:
            if w.w.get(ev[0], 0) < ev[1]:
                w.w[ev[0]] = ev[1]

    def op(self, E, fn, reads=(), writes=()):
        self._wait(E, self._deps(reads, writes))
        inst = fn(E.eng)
        E.count += 1
        inst.then_inc(E.sem, 1)
        self.ninst += 1
        self._finish((E.sem.num, E.count), reads, writes)

    def mm(self, out_res, mms, reads, transpose=False):
        E = self.pe
        self._wait(E, self._deps(reads, [out_res]), skip_own=True)
        inst = None
        for (o, a, b, st, sp) in mms:
            if transpose:
                inst = E.eng.transpose(out=o, in_=a, identity=b)
            else:
                inst = E.eng.matmul(o, lhsT=a, rhs=b, start=st, stop=sp)
            self.ninst += 1
        E.count += 1
        inst.then_inc(E.sem, 1)
        self._finish((E.sem.num, E.count), reads, [out_res])

    def dma(self, Q, key, pairs, reads=(), writes=(), **kw):
        if key not in self.dsem:
            if self.free_dsems:
                self.dsem[key] = self.free_dsems.pop()
            else:
                sem = self.nc.alloc_semaphore(U("d_" + key))
                self.sems[sem.num] = sem
                self.dsem[key] = [sem, 0]
        sem, cnt = self.dsem[key]
        deps = self._deps(reads, writes)
        if cnt > 0 and deps.get(sem.num, 0) < cnt:
            deps[sem.num] = cnt
        self._wait(Q, deps)
        for (o, i) in pairs:
            Q.eng.dma_start(out=o, in_=i, **kw).then_inc(sem, 16)
            cnt += 16
            self.ninst += 1
        self.dsem[key][1] = cnt
        self._finish((sem.num, cnt), reads, writes)

    def collective(self, key, kind, ins, outs, rg, reads=(), writes=()):
        Q = self.pool
        if key not in self.dsem:
            if self.free_dsems:
                self.dsem[key] = self.free_dsems.pop()
            else:
                sem = self.nc.alloc_semaphore(U("c_" + key))
                self.sems[sem.num] = sem
                self.dsem[key] = [sem, 0]
        sem, cnt = self.dsem[key]
        deps = self._deps(reads, writes)
        if cnt > 0 and deps.get(sem.num, 0) < cnt:
            deps[sem.num] = cnt
        self._wait(Q, deps)
        Q.eng.collective_compute(kind, ALU.bypass, replica_groups=rg, ins=ins, outs=outs).then_inc(sem, 1)
        cnt += 1
        self.ninst += 1
        self.dsem[key][1] = cnt
        self._finish((sem.num, cnt), reads, writes)

    def barrier(self, final=False):
        deps = {E.sem.num: E.count for E in self.engs if E.count > 0}
        for key, (sem, cnt) in self.dsem.items():
            if cnt > 0 and (final or not key.startswith(("cv", "ag"))):
                deps[sem.num] = cnt
        self._wait(self.sp, deps)
        if not final:
            for key in list(self.dsem):
                if not key.startswith(("cv", "ag")):
                    self.free_dsems.append(self.dsem.pop(key))
        self.barcount += 1
        self.sp.eng.sem_inc(self.bar, 1)
        for E in self.engs:
            if E is not self.sp:
                E.eng.wait_ge(self.bar, self.barcount)
            for k, v in deps.items():
                E.known[k] = max(E.known.get(k, 0), v)


def _consts():
    f = np.float32
    pos = np.arange(2048)
    r = (pos // 64).astype(f)
    cl = (pos % 64).astype(f)
    inv = (10000.0 ** (-np.arange(32, dtype=f) / 32)).astype(f)
    ar = r[:, None] * inv
    ac = cl[:, None] * inv
    ang = np.concatenate([ar, ar, ac, ac], -1)
    cos = np.cos(ang).astype(f)
    sin = np.sin(ang).astype(f)
    sgn = np.concatenate([-np.ones(32), np.ones(32), -np.ones(32), np.ones(32)]).astype(f)
    cosT = np.ones((128, TOK), f)
    sinT = np.zeros((128, TOK), f)
    cosT[:, 256:] = cos.T
    sinT[:, 256:] = (sin * sgn[None, :]).T
    p = np.arange(128)
    kk = p[:, None]
    qq = p[None, :]
    mprev = np.tile((kk >= qq).astype(f), (1, 4)).astype(ml_dtypes.bfloat16)
    mnext = np.tile((kk <= qq).astype(f), (1, 4)).astype(ml_dtypes.bfloat16)
    s_ = p[:, None]
    c_ = p[None, :]
    relF = np.maximum(c_ - s_, 0).astype(f)
    mF = (c_ >= s_).astype(f)
    relB = np.maximum(s_ - c_, 0).astype(f)
    mB = (c_ < s_).astype(f)
    rows = np.stack([np.broadcast_to((c_ + 1).astype(f), (128, 128)),
                     np.broadcast_to((128 - c_).astype(f), (128, 128))], 0)
    rett = np.ascontiguousarray(np.stack([relF, mF * SCALE, relB, mB * SCALE, rows[0], rows[1]], 1)).astype(f)
    pcol = np.stack([127 - p, p, np.full(128, 128)], 1).astype(f)
    ident = np.eye(128, dtype=f).astype(ml_dtypes.bfloat16)
    ones = np.ones((128, 128), f).astype(ml_dtypes.bfloat16)
    return dict(cosT=cosT, sinT=sinT, mprev=mprev, mnext=mnext, rett=rett, pcol=pcol, ident=ident, onesb=ones)


class Ctx:
    pass


_UN = [0]


def U(name):
    _UN[0] += 1
    return f"{name}_{_UN[0]}"


def build(stop_after=None, dump=(), inject=(), only=None):
    nc = bass.Bass("TRN2", target_bir_lowering=False)
    fw = FW(nc)
    C = Ctx()
    C.nc = nc
    C.fw = fw
    dump = set(dump)

    def dram_in(name, shape, dt):
        return nc.dram_tensor(name, list(shape), dt, kind="ExternalInput").ap()

    inject = set(inject)

    def scratch(name, shape, dt):
        kind = "ExternalOutput" if name in dump else ("ExternalInput" if name in inject else "Internal")
        return nc.dram_tensor(name, list(shape), dt, kind=kind).ap()

    C.x_in = dram_in("x", [2048, D], F32)
    C.ctx_in = dram_in("ctx", [256, D], F32)
    C.c9T = dram_in("c9T", [128, KC, 9], F32)
    C.sel = dram_in("sel", [9, 2], F32)
    small = only is not None and "convert" not in only
    C.w_ada = dram_in("w_ada_sh", [DEPTH, 128 if small else D, 3072], F32)
    C.b_ada = dram_in("b_ada_sh", [DEPTH, 3072], F32)
    C.norm_g = dram_in("norm_g", [DEPTH, 4, D], F32)
    C.norm_gT = dram_in("norm_gT", [DEPTH, 4, 128, KC], F32)
    C.w_in = dram_in("w_in_sh", [DEPTH, 128 if small else D // 8, 9216], F32)
    C.attn_sink = dram_in("attn_sink", [DEPTH, 16], F32)
    C.ret_log_decay = dram_in("ret_log_decay", [DEPTH, 16], F32)
    C.ret_gn_g = dram_in("ret_gn_g", [DEPTH, 8, 256], F32)
    C.w_out = dram_in("w_out_sh", [DEPTH, 128 if small else D // 8, D], F32)
    C.w_up = dram_in("w_up_sh", [DEPTH, 128 if small else D // 8, DFF], F32)
    C.w_down = dram_in("w_down_sh", [DEPTH, 128 if small else DFF // 8, D], F32)
    C.k_cosT = dram_in("cosT", [128, TOK], F32)
    C.k_sinT = dram_in("sinT", [128, TOK], F32)
    C.k_mprev = dram_in("mprev", [128, 512], BF16)
    C.k_mnext = dram_in("mnext", [128, 512], BF16)
    C.k_rett = dram_in("rett", [128, 6, 128], F32)
    C.k_pcol = dram_in("pcol", [128, 3], F32)
    C.k_ident = dram_in("ident", [128, 128], BF16)
    C.k_onesb = dram_in("onesb", [128, 128], BF16)
    C.out = nc.dram_tensor("out", [2048, D], F32, kind="ExternalOutput").ap()

    WSPEC = [("w_in", D, 9216), ("w_out", D, D), ("w_up", D, DFF), ("w_down", DFF, D)]
    C.wspec = WSPEC
    C.wsh = {k: nc.dram_tensor(k + "_shb", [DEPTH, rows // 8, cols], BF16, kind="Internal").ap() for k, rows, cols in WSPEC}
    C.wb = {k: [(nc.dram_tensor(f"{k}_b{li}", [rows, cols], BF16, kind="ExternalInput").ap() if f"{k}_b{li}" in inject else
                 nc.dram_tensor(f"{k}_b{li}", [rows, cols], BF16, kind="Internal", addr_space="Shared").ap()) for li in range(DEPTH)]
            for k, rows, cols in WSPEC}
    C.wres = {k: [[Res(ro=True) for _ in range(rows // 1024)] for _ in range(DEPTH)] for k, rows, cols in WSPEC}
    C.modp = nc.dram_tensor("modp", [DEPTH, 9, 3072], F32, kind="Internal").ap()
    C.modg = [nc.dram_tensor(f"modg{li}", [72, 3072], F32, kind="Internal", addr_space="Shared").ap() for li in range(DEPTH)]
    C.mod_s = scratch("mod_s", [DEPTH, 2, 6 * D], F32)
    C.mod_res = [Res() for _ in range(DEPTH)]
    C.hT_s = scratch("hT_s", [NT, 128, KC, 128], BF16)
    C.hT_res = [Res() for _ in range(NT)]
    C.h2T_s = scratch("h2T_s", [NT, 128, KC, 128], BF16)
    C.h2T_res = [Res() for _ in range(NT)]
    C.mixT_s = scratch("mixT_s", [NT, 128, KC, 128], BF16)
    C.mix_res = [Res() for _ in range(NT)]
    C.aqT_s = scratch("aqT_s", [16, 128, TOK], BF16)
    C.akT_s = scratch("akT_s", [4, 128, TOK], BF16)
    C.rqT_s = scratch("rqT_s", [8, 128, TOK], BF16)
    C.rkT_s = scratch("rkT_s", [8, 128, TOK], BF16)
    C.av_s = scratch("av_s", [TOK, 512], BF16)
    C.rv_s = scratch("rv_s", [TOK, 2048], BF16)
    C.sg_s = scratch("sg_s", [TOK, 2048], BF16)
    C.proj_res = Res()
    C.y_s = scratch("y_s", [TOK, D], F32)
    C.y_res = [Res() for _ in range(NT)]
    C.xmid_s = scratch("xmid_s", [TOK, D], F32)
    C.xmid_res = [Res() for _ in range(NT)]
    C.xnext_s = scratch("xnext_s", [TOK, D], F32)
    C.xnext_res = [Res() for _ in range(NT)]

    phases = []

    def run(name, fn, *a):
        if C.stopped or (only is not None and name not in only):
            return
        fn(C, *a)
        fw.barrier()
        if stop_after == name:
            C.stopped = True

    C.stopped = False
    with ExitStack() as _es:
        ident = _es.enter_context(nc.sbuf_tensor(U("ident"), [128, 128], BF16))
        ssq = _es.enter_context(nc.sbuf_tensor(U("ssq"), [128, NT, 8], F32))
        C.ident = ident
        C.ident_res = Res(ro=True)
        C.ssq = ssq
        C.ssq_res = Res()
        fw.dma(fw.sp, "const0", [(ident[:], C.k_ident)], writes=[C.ident_res])
        if "ssq_in" in inject:
            fw.dma(fw.sp, "const1", [(ssq[:], dram_in("ssq_in", [128, NT, 8], F32))], writes=[C.ssq_res])
        run("convert", phase_convert, 0, 4)
        run("mod", phase_mod)
        run("convert", phase_convert, 4, None)
        for li in range(DEPTH):
            last = li == DEPTH - 1
            C.li = li
            C.last = last
            C.tiles = list(range(2, NT)) if last else list(range(NT))
            if li == 0:
                run("norm1_0", phase_resid_norm, None, None, None, 0, C.hT_s, C.hT_res, list(range(NT)))
            run(f"inproj_{li}", phase_inproj)
            run(f"ret_{li}", phase_ret)
            run(f"att_{li}", phase_att)
            run(f"outproj_{li}", phase_proj_out, "w_out", C.mixT_s, C.mix_res)
            run(f"resid1_{li}", phase_resid_norm, "x", C.xmid_s, C.xmid_res, 1, C.h2T_s, C.h2T_res, C.tiles)
            run(f"mlp_{li}", phase_mlp)
            if not last:
                run(f"resid2_{li}", phase_resid_norm, "xmid", C.xnext_s, C.xnext_res, 2, C.hT_s, C.hT_res, C.tiles)
            else:
                run(f"resid2_{li}", phase_resid_norm, "xmid", "out", None, 2, None, None, C.tiles)
        fw.barrier(final=True)
    return nc


RG = [list(range(8))]


def kperm(name, s_):
    rs = (DFF // 8 if name == "w_down" else D // 8) // 128
    return (s_ % 8) * rs + s_ // 8


def phase_convert(C, lo, hi):
    fw = C.fw
    srcs = {"w_in": C.w_in, "w_out": C.w_out, "w_up": C.w_up, "w_down": C.w_down}
    items = [(li, name, c) for li in range(DEPTH) for name, rows, cols in C.wspec for c in range(rows // 1024)]
    for i, (li, name, c) in enumerate(items):
        if not (lo <= i < (hi if hi is not None else len(items))):
            continue
        src, sh, dst = srcs[name], C.wsh[name], C.wb[name][li]
        r_sh = Res()
        fw.dma(fw.pool, f"cv{i % 8}", [(sh[li, c * 128:(c + 1) * 128, :], src[li, c * 128:(c + 1) * 128, :])], writes=[r_sh])
        fw.collective(f"ag{i % 8}", "AllGather", [sh[li, c * 128:(c + 1) * 128, :]], [dst[c * 1024:(c + 1) * 1024, :]], RG,
                      reads=[r_sh], writes=[C.wres[name][li][c]])


def phase_mod(C):
    nc, fw = C.nc, C.fw
    with ExitStack() as _es:
        cT = _es.enter_context(nc.sbuf_tensor(U("m_c"), [128, KC, 9], F32))
        cs = _es.enter_context(nc.sbuf_tensor(U("m_cs"), [128, KC, 9], F32))
        wt = _es.enter_context(nc.sbuf_tensor(U("m_w"), [128, 2, KC, 512], F32))
        bt = _es.enter_context(nc.sbuf_tensor(U("m_b"), [9, 2, 512], F32))
        ot = _es.enter_context(nc.sbuf_tensor(U("m_o"), [9, 2, 512], F32))
        sel = _es.enter_context(nc.sbuf_tensor(U("m_sel"), [9, 2], F32))
        gt = _es.enter_context(nc.sbuf_tensor(U("m_g"), [9, 2, 3072], F32))
        o2 = _es.enter_context(nc.sbuf_tensor(U("m_o2"), [2, 2, 512], F32))
        ps0 = _es.enter_context(nc.psum_tensor(U("m_ps0"), [128, 512], F32))
        ps1 = _es.enter_context(nc.psum_tensor(U("m_ps1"), [128, 512], F32))
        r_c, r_cs, r_sel = Res(), Res(), Res()
        r_w, r_b, r_o, r_ps, r_g, r_o2 = [Res(), Res()], [Res(), Res()], [Res(), Res()], [Res(), Res()], [Res(), Res()], [Res(), Res()]
        r_modp = [Res() for _ in range(DEPTH)]
        r_modg = [Res() for _ in range(DEPTH)]
        pss = [ps0, ps1]
        fw.dma(fw.sp, "m_c", [(cT[:], C.c9T), (sel[:], C.sel)], writes=[r_c, r_sel])
        fw.op(fw.act, lambda e: e.activation(out=cs[:], in_=cT[:], func=AF.Silu), reads=[r_c], writes=[r_cs])
        n = 0
        for li in range(DEPTH):
            for cb in range(6):
                s = n % 2
                n += 1
                cols = slice(cb * 512, (cb + 1) * 512)
                fw.dma(fw.sp, f"m_w{s}", [(wt[:, s], C.w_ada[li, :, cols].rearrange("(k p) n -> p k n", p=128))], writes=[r_w[s]])
                fw.dma(fw.sp, f"m_b{s}", [(bt[:, s, :], C.b_ada[li, cols].partition_broadcast(9))], writes=[r_b[s]])
                fw.mm(r_ps[s], [(pss[s][0:9, :], cs[:, kc, :], wt[:, s, kc, :], kc == 0, kc == KC - 1) for kc in range(KC)],
                      reads=[r_cs, r_w[s]])
                fw.op(fw.dve, lambda e: e.tensor_tensor(out=ot[:, s, :], in0=pss[s][0:9, :], in1=bt[:, s, :], op=ALU.add),
                      reads=[r_ps[s], r_b[s]], writes=[r_o[s]])
                fw.dma(fw.sp, f"m_o{s}", [(C.modp[li, :, cols], ot[:, s, :])], reads=[r_o[s]], writes=[r_modp[li]])
            fw.collective(f"mg{li}", "AllGather", [C.modp[li]], [C.modg[li]], RG, reads=[r_modp[li]], writes=[r_modg[li]])
        m = 0
        for li in range(DEPTH):
            for r in range(8):
                g = r % 2
                fw.dma(fw.sp, f"m_g{g}", [(gt[:, g, :], C.modg[li][r * 9:(r + 1) * 9, :])], reads=[r_modg[li]], writes=[r_g[g]])
                for j in range(6):
                    s = m % 2
                    m += 1
                    fw.mm(r_ps[s], [(pss[s][0:2, :], sel[:], gt[:, g, j * 512:(j + 1) * 512], True, True)], reads=[r_sel, r_g[g]])
                    fw.op(fw.dve, lambda e: e.tensor_copy(out=o2[:, s, :], in_=pss[s][0:2, :]), reads=[r_ps[s]], writes=[r_o2[s]])
                    fw.dma(fw.sp, f"m_p{s}", [(C.mod_s[li, :, j * D + r * 512:j * D + (r + 1) * 512], o2[:, s, :])], reads=[r_o2[s]],
                           writes=[C.mod_res[li]])


def load_mod_tiles(C, which, AB, r_AB, G, r_G, tmp, r_tmp):
    nc, fw = C.nc, C.fw
    li = C.li
    if which == 0:
        gi, (mli, shi, sci, ng) = None, (li, 0, 1, 0)
    elif which == 1:
        gi, (mli, shi, sci, ng) = (li, 2, 1), (li, 3, 4, 2)
    else:
        gi = (li, 5, 3)
        mli, shi, sci, ng = (li + 1, 0, 1, 0) if li + 1 < DEPTH else (None, 0, 0, 0)
    if mli is not None:
        pairs = []
        for row in range(2):
            pairs.append((AB[:, row, 1, :], C.mod_s[mli, row, shi * D:(shi + 1) * D].rearrange("(k p) -> p k", p=128)))
            pairs.append((AB[:, row, 0, :], C.mod_s[mli, row, sci * D:(sci + 1) * D].rearrange("(k p) -> p k", p=128)))
        fw.dma(fw.sp, "mt_ab", pairs, reads=[C.mod_res[mli]], writes=[r_AB], allow_slow_non_contiguous=True)
        fw.dma(fw.sp, "mt_g", [(tmp[:, 0:KC], C.norm_gT[mli, ng])], writes=[r_tmp])
        for row in range(2):
            fw.op(fw.dve, lambda e: e.scalar_tensor_tensor(out=AB[:, row, 0, :], in0=AB[:, row, 0, :], scalar=1.0, in1=tmp[:, 0:KC],
                                                            op0=ALU.add, op1=ALU.mult), reads=[r_tmp], writes=[r_AB])
    if gi is not None:
        gli, gti, gng = gi
        for row in range(2):
            fw.dma(fw.sp, "mt_G", [(G[:, row, :], C.mod_s[gli, row, gti * D:(gti + 1) * D].partition_broadcast(128))],
                   reads=[C.mod_res[gli]], writes=[r_G])
        fw.dma(fw.sp, "mt_g2", [(tmp[:], C.norm_g[gli, gng].partition_broadcast(128))], reads=[], writes=[r_tmp])
        for row in range(2):
            fw.op(fw.dve, lambda e: e.tensor_tensor(out=G[:, row, :], in0=G[:, row, :], in1=tmp[:], op=ALU.mult),
                  reads=[r_tmp], writes=[r_G])


def x_src(C, which, t):
    rows = slice(t * 128, (t + 1) * 128)
    if which == "x":
        if C.li == 0:
            return (C.ctx_in[rows, :], None) if t < 2 else (C.x_in[(t - 2) * 128:(t - 1) * 128, :], None)
        return C.xnext_s[rows, :], C.xnext_res[t]
    return C.xmid_s[rows, :], C.xmid_res[t]


def phase_resid_norm(C, xwhich, xdst, xdst_res, which, hdst, hdst_res, tiles):
    nc, fw = C.nc, C.fw
    li = C.li
    resid = xwhich is not None
    do_norm = hdst is not None
    with ExitStack() as _es:
        AB = _es.enter_context(nc.sbuf_tensor(U("rn_AB"), [128, 2, 2, KC], F32))
        G = _es.enter_context(nc.sbuf_tensor(U("rn_G"), [128, 2, D], F32))
        tmp = _es.enter_context(nc.sbuf_tensor(U("rn_tmp"), [128, D], F32))
        xt = _es.enter_context(nc.sbuf_tensor(U("rn_x"), [128, 2, D], F32))
        yt = _es.enter_context(nc.sbuf_tensor(U("rn_y"), [128, 2, D], F32))
        junk = _es.enter_context(nc.sbuf_tensor(U("rn_junk"), [128, D], BF16))
        xs = _es.enter_context(nc.sbuf_tensor(U("rn_xs"), [128, 2, D], BF16))
        hT = _es.enter_context(nc.sbuf_tensor(U("rn_hT"), [128, 2, KC, 128], BF16))
        st = _es.enter_context(nc.sbuf_tensor(U("rn_st"), [128, 2, 8], F32))
        p0 = _es.enter_context(nc.psum_tensor(U("rn_p0"), [128, 1024], BF16))
        p1 = _es.enter_context(nc.psum_tensor(U("rn_p1"), [128, 1024], BF16))
        r_AB, r_G, r_tmp, r_junk = Res(), Res(), Res(), Res()
        r_x, r_y, r_xs, r_hT, r_st = [Res(), Res()], [Res(), Res()], [Res(), Res()], [Res(), Res()], [Res(), Res()]
        r_p = [Res(), Res()]
        pp = [p0, p1]
        load_mod_tiles(C, which, AB, r_AB, G if resid else None, r_G, tmp, r_tmp)
        def issue_loads(n):
            t = tiles[n]
            s = n % 2
            rows = slice(t * 128, (t + 1) * 128)
            if resid:
                xap, xres = x_src(C, xwhich, t)
                fw.dma(fw.sp, f"rn_x{s}", [(xt[:, s, :], xap)], reads=[xres] if xres else [], writes=[r_x[s]])
                fw.dma(fw.sp, f"rn_y{s}", [(yt[:, s, :], C.y_s[rows, :])], reads=[C.y_res[t]], writes=[r_y[s]])
            else:
                xap = C.ctx_in[rows, :] if t < 2 else C.x_in[(t - 2) * 128:(t - 1) * 128, :]
                fw.dma(fw.sp, f"rn_x{s}", [(xt[:, s, :], xap)], writes=[r_x[s]])

        issue_loads(0)
        for n, t in enumerate(tiles):
            s = n % 2
            row = 0 if t >= 2 else 1
            rows = slice(t * 128, (t + 1) * 128)
            if n + 1 < len(tiles):
                issue_loads(n + 1)
            if resid:
                fw.op(fw.dve, lambda e: e.tensor_reduce(out=st[:, s, 0:1], in_=C.ssq[:, t, :], axis=AX.X, op=ALU.add),
                      reads=[C.ssq_res], writes=[r_st[s]])
                fw.op(fw.dve, lambda e: e.tensor_scalar(out=st[:, s, 1:2], in0=st[:, s, 0:1], scalar1=1.0 / D, scalar2=EPS, op0=ALU.mult, op1=ALU.add),
                      reads=[r_st[s]], writes=[r_st[s]])
                fw.op(fw.act, lambda e: e.activation(out=st[:, s, 2:3], in_=st[:, s, 1:2], func=AF.Sqrt), reads=[r_st[s]], writes=[r_st[s]])
                fw.op(fw.dve, lambda e: e.reciprocal(out=st[:, s, 3:4], in_=st[:, s, 2:3]), reads=[r_st[s]], writes=[r_st[s]])
                fw.op(fw.dve, lambda e: e.scalar_tensor_tensor(out=yt[:, s, :], in0=yt[:, s, :], scalar=st[:, s, 3:4], in1=G[:, row, :],
                                                                op0=ALU.mult, op1=ALU.mult), reads=[r_st[s], r_G, r_y[s]], writes=[r_y[s]])
                fw.op(fw.dve, lambda e: e.tensor_tensor(out=xt[:, s, :], in0=xt[:, s, :], in1=yt[:, s, :], op=ALU.add),
                      reads=[r_y[s], r_x[s]], writes=[r_x[s]])
                if xdst == "out":
                    fw.dma(fw.sp, f"rn_xo{s}", [(C.out[(t - 2) * 128:(t - 1) * 128, :], xt[:, s, :])], reads=[r_x[s]])
                else:
                    fw.dma(fw.sp, f"rn_xo{s}", [(xdst[rows, :], xt[:, s, :])], reads=[r_x[s]], writes=[xdst_res[t]])
            if not do_norm:
                continue
            fw.op(fw.act, lambda e: e.activation(out=junk[:], in_=xt[:, s, :], func=AF.Square, accum_out=st[:, s, 4:5]),
                  reads=[r_x[s]], writes=[r_junk, r_st[s]])
            fw.op(fw.dve, lambda e: e.tensor_scalar(out=st[:, s, 5:6], in0=st[:, s, 4:5], scalar1=1.0 / D, scalar2=EPS, op0=ALU.mult, op1=ALU.add),
                  reads=[r_st[s]], writes=[r_st[s]])
            fw.op(fw.act, lambda e: e.activation(out=st[:, s, 6:7], in_=st[:, s, 5:6], func=AF.Sqrt), reads=[r_st[s]], writes=[r_st[s]])
            fw.op(fw.dve, lambda e: e.reciprocal(out=st[:, s, 7:8], in_=st[:, s, 6:7]), reads=[r_st[s]], writes=[r_st[s]])
            fw.op(fw.dve, lambda e: e.tensor_scalar(out=xs[:, s, :], in0=xt[:, s, :], scalar1=st[:, s, 7:8], scalar2=None, op0=ALU.mult),
                  reads=[r_st[s], r_x[s]], writes=[r_xs[s]])
            for g8 in range(4):
                b = g8 % 2
                fw.mm(r_p[b], [(pp[b][:, j * 128:(j + 1) * 128], xs[:, s, (g8 * 8 + j) * 128:(g8 * 8 + j + 1) * 128], C.ident[:], True, True)
                               for j in range(8)], reads=[r_xs[s], C.ident_res], transpose=True)
                for j in range(8):
                    kc = g8 * 8 + j
                    E = fw.dve if b == 0 else fw.act
                    if E is fw.dve:
                        fw.op(E, lambda e: e.tensor_scalar(out=hT[:, s, kc, :], in0=pp[b][:, j * 128:(j + 1) * 128],
                                                           scalar1=AB[:, row, 0, kc:kc + 1], scalar2=AB[:, row, 1, kc:kc + 1],
                                                           op0=ALU.mult, op1=ALU.add), reads=[r_p[b], r_AB], writes=[r_hT[s]])
                    else:
                        fw.op(E, lambda e: e.activation(out=hT[:, s, kc, :], in_=pp[b][:, j * 128:(j + 1) * 128], func=AF.Identity,
                                                        scale=AB[:, row, 0, kc:kc + 1], bias=AB[:, row, 1, kc:kc + 1]),
                              reads=[r_p[b], r_AB], writes=[r_hT[s]])
            fw.dma(fw.sp, f"rn_ho{s}", [(hdst[t], hT[:, s])], reads=[r_hT[s]], writes=[hdst_res[t]])


def phase_inproj(C):
    nc, fw = C.nc, C.fw
    li = C.li
    wsrc = C.wb["w_in"]
    blocks = [list(range(0, 9)), list(range(9, 18))]
    FM = {0: (C.aqT_s, 0), 1: (C.aqT_s, 4), 2: (C.aqT_s, 8), 3: (C.aqT_s, 12), 4: (C.akT_s, 0),
          6: (C.rqT_s, 0), 7: (C.rqT_s, 4), 8: (C.rkT_s, 0), 9: (C.rkT_s, 4)}
    TM = {5: (C.av_s, 0, AF.Copy), 10: (C.rv_s, 0, AF.Copy), 11: (C.rv_s, 512, AF.Copy), 12: (C.rv_s, 1024, AF.Copy),
          13: (C.rv_s, 1536, AF.Copy), 14: (C.sg_s, 0, AF.Silu), 15: (C.sg_s, 512, AF.Silu), 16: (C.sg_s, 1024, AF.Silu),
          17: (C.sg_s, 1536, AF.Silu)}
    with ExitStack() as _es:
        hb = _es.enter_context(nc.sbuf_tensor(U("ip_h"), [128, 9, KC, 128], BF16))
        wt = _es.enter_context(nc.sbuf_tensor(U("ip_w"), [128, 2, KC, 512], BF16))
        cos = _es.enter_context(nc.sbuf_tensor(U("ip_cos"), [128, TOK], F32))
        sin = _es.enter_context(nc.sbuf_tensor(U("ip_sin"), [128, TOK], F32))
        t1 = _es.enter_context(nc.sbuf_tensor(U("ip_t1"), [128, 2, 384], F32))
        t2 = _es.enter_context(nc.sbuf_tensor(U("ip_t2"), [128, 2, 384], F32))
        stg = _es.enter_context(nc.sbuf_tensor(U("ip_st"), [128, 4, 512], BF16))
        p0 = _es.enter_context(nc.psum_tensor(U("ip_p0"), [128, 512], F32))
        p1 = _es.enter_context(nc.psum_tensor(U("ip_p1"), [128, 512], F32))
        p2 = _es.enter_context(nc.psum_tensor(U("ip_p2"), [128, 512], F32))
        p3 = _es.enter_context(nc.psum_tensor(U("ip_p3"), [128, 512], F32))
        pp = [p0, p1, p2, p3]
        r_p = [Res() for _ in range(4)]
        r_h = Res()
        r_w = [Res(), Res()]
        r_cs = Res(ro=True)
        r_t1, r_t2 = [Res(), Res()], [Res(), Res()]
        r_stg = [Res() for _ in range(4)]
        fw.dma(fw.sp, "ip_cs", [(cos[:], C.k_cosT), (sin[:], C.k_sinT)], writes=[r_cs])
        wn = 0
        un = 0

        def load_w(g, slot):
            fw.dma(fw.sp, f"ip_w{slot}", [(wt[:, slot], wsrc[li][:, g * 512:(g + 1) * 512].rearrange("(k p) n -> p k n", p=128))],
                   reads=C.wres["w_in"][li], writes=[r_w[slot]])

        for blk in blocks:
            fw.dma(fw.sp, "ip_h", [(hb[:, i], C.hT_s[t]) for i, t in enumerate(blk)], reads=[C.hT_res[t] for t in blk], writes=[r_h])
            load_w(0, wn % 2)
            for g in range(18):
                slot = wn % 2
                wn += 1
                if g + 1 < 18:
                    load_w(g + 1, wn % 2)
                if g in FM:
                    dst, h0 = FM[g]
                    for j in range(4):
                        for sub in range(3):
                            tok0 = (blk[0] + 3 * sub) * 128
                            b = un % 4
                            u2 = un % 2
                            un += 1
                            fw.mm(r_p[b], [(pp[b][:, 0:384], wt[:, slot, kc, j * 128:(j + 1) * 128], hb[:, 3 * sub:3 * sub + 3, kperm("w_in", kc), :], kc == 0, kc == KC - 1)
                                           for kc in range(KC)], reads=[r_w[slot], r_h])
                            ps = pp[b]
                            cs_ = cos[:, tok0:tok0 + 384]
                            fw.op(fw.dve, lambda e: e.tensor_tensor(out=t1[:, u2, :], in0=ps[:, 0:384], in1=cs_, op=ALU.mult),
                                  reads=[r_p[b], r_cs], writes=[r_t1[u2]])
                            for q in range(4):
                                src = q * 32 + 32 if q % 2 == 0 else q * 32 - 32
                                fw.op(fw.dve, lambda e: e.tensor_tensor(out=t2[q * 32:q * 32 + 32, u2, :], in0=ps[src:src + 32, 0:384],
                                                                        in1=sin[q * 32:q * 32 + 32, tok0:tok0 + 384], op=ALU.mult),
                                      reads=[r_p[b], r_cs], writes=[r_t2[u2]])
                            fw.op(fw.dve, lambda e: e.tensor_tensor(out=stg[:, b, 0:384], in0=t1[:, u2, :], in1=t2[:, u2, :], op=ALU.add),
                                  reads=[r_t1[u2], r_t2[u2]], writes=[r_stg[b]])
                            fw.dma(fw.sp, f"ip_o{b}", [(dst[h0 + j, :, tok0:tok0 + 384], stg[:, b, 0:384])], reads=[r_stg[b]], writes=[C.proj_res])
                else:
                    dst, c0, func = TM[g]
                    for i, t in enumerate(blk):
                        b = un % 4
                        un += 1
                        fw.mm(r_p[b], [(pp[b][:], hb[:, i, kperm("w_in", kc), :], wt[:, slot, kc, :], kc == 0, kc == KC - 1) for kc in range(KC)],
                              reads=[r_w[slot], r_h])
                        ps = pp[b]
                        fw.op(fw.act, lambda e: e.activation(out=stg[:, b, :], in_=ps[:], func=func), reads=[r_p[b]], writes=[r_stg[b]])
                        fw.dma(fw.sp, f"ip_o{b}", [(dst[t * 128:(t + 1) * 128, c0:c0 + 512], stg[:, b, :])], reads=[r_stg[b]], writes=[C.proj_res])


def phase_ret(C):
    nc, fw = C.nc, C.fw
    li = C.li
    out_tiles = set(C.tiles)
    with ExitStack() as _es:
        tab = _es.enter_context(nc.sbuf_tensor(U("rt_tab"), [128, 6, 128], F32))
        pcol = _es.enter_context(nc.sbuf_tensor(U("rt_pcol"), [128, 3], F32))
        lg = _es.enter_context(nc.sbuf_tensor(U("rt_lg"), [128, 16], F32))
        DT = _es.enter_context(nc.sbuf_tensor(U("rt_DT"), [128, 8, 128], F32))
        dtmp = _es.enter_context(nc.sbuf_tensor(U("rt_tmp"), [128, 2, 128], F32))
        qrow = _es.enter_context(nc.sbuf_tensor(U("rt_qrow"), [128, 8, 2, 128], F32))
        kd = _es.enter_context(nc.sbuf_tensor(U("rt_kd"), [128, 8, 4], F32))
        gn = _es.enter_context(nc.sbuf_tensor(U("rt_gn"), [128, 8, 256], F32))
        qT = _es.enter_context(nc.sbuf_tensor(U("rt_q"), [128, 2, TOK], BF16))
        kT = _es.enter_context(nc.sbuf_tensor(U("rt_k"), [128, 2, TOK], BF16))
        vv = _es.enter_context(nc.sbuf_tensor(U("rt_v"), [128, 2, NT, 256], BF16))
        gg = _es.enter_context(nc.sbuf_tensor(U("rt_g"), [128, 2, NT, 256], BF16))
        kf = _es.enter_context(nc.sbuf_tensor(U("rt_kf"), [128, NT, 128], BF16))
        kb = _es.enter_context(nc.sbuf_tensor(U("rt_kb"), [128, 2, 128], BF16))
        S = _es.enter_context(nc.sbuf_tensor(U("rt_S"), [128, 2, 256], F32))
        Sfb = _es.enter_context(nc.sbuf_tensor(U("rt_Sfb"), [128, 2, 256], BF16))
        Sbb = _es.enter_context(nc.sbuf_tensor(U("rt_Sbb"), [128, NT, 256], BF16))
        A = _es.enter_context(nc.sbuf_tensor(U("rt_A"), [128, 2, 128], BF16))
        qd = _es.enter_context(nc.sbuf_tensor(U("rt_qd"), [128, 2, 2, 128], BF16))
        st = _es.enter_context(nc.sbuf_tensor(U("rt_st"), [128, 2, 12], F32))
        yn = _es.enter_context(nc.sbuf_tensor(U("rt_yn"), [128, 2, 256], F32))
        yb = _es.enter_context(nc.sbuf_tensor(U("rt_yb"), [128, 2, 256], BF16))
        ro = _es.enter_context(nc.sbuf_tensor(U("rt_o"), [128, 2, 2, NT, 128], BF16))
        pk = _es.enter_context(nc.psum_tensor(U("rt_pk"), [128, 1024], BF16))
        pt = _es.enter_context(nc.psum_tensor(U("rt_pt"), [128, 1024], BF16))
        psc = _es.enter_context(nc.psum_tensor(U("rt_ps"), [128, 512], F32))
        py0 = _es.enter_context(nc.psum_tensor(U("rt_py0"), [128, 512], F32))
        py1 = _es.enter_context(nc.psum_tensor(U("rt_py1"), [128, 512], F32))
        pkv0 = _es.enter_context(nc.psum_tensor(U("rt_pkv0"), [128, 512], F32))
        pkv1 = _es.enter_context(nc.psum_tensor(U("rt_pkv1"), [128, 512], F32))
        r_tab, r_lg, r_DT, r_dtmp, r_qrow, r_kd, r_gn = Res(ro=True), Res(), Res(), Res(), Res(), Res(), Res()
        r_in = [Res(), Res()]
        r_kf, r_S, r_Sbb = Res(), Res(), Res()
        r_Sfb = [Res(), Res()]
        r_kb = [Res(), Res()]
        r_A, r_qd, r_st, r_yn, r_yb = [Res(), Res()], [Res(), Res()], [Res(), Res()], [Res(), Res()], [Res(), Res()]
        r_ro = [Res(), Res()]
        r_pk, r_pt, r_psc = Res(), Res(), Res()
        r_py, r_pkv = [Res(), Res()], [Res(), Res()]
        py, pkv = [py0, py1], [pkv0, pkv1]
        fw.dma(fw.sp, "rt_c", [(tab[:], C.k_rett), (pcol[:], C.k_pcol)], writes=[r_tab])
        fw.dma(fw.sp, "rt_lg", [(lg[:], C.ret_log_decay[li].partition_broadcast(128))], writes=[r_lg])
        fw.dma(fw.sp, "rt_gn", [(gn[:, h, :], C.ret_gn_g[li, h].partition_broadcast(128)) for h in range(8)], writes=[r_gn])
        for h in range(8):
            lf = lg[:, h:h + 1]
            lb = lg[:, 8 + h:8 + h + 1]
            fw.op(fw.act, lambda e: e.activation(out=dtmp[:, 0, :], in_=tab[:, 0, :], func=AF.Exp, scale=lf), reads=[r_tab, r_lg], writes=[r_dtmp])
            fw.op(fw.act, lambda e: e.activation(out=dtmp[:, 1, :], in_=tab[:, 2, :], func=AF.Exp, scale=lb), reads=[r_tab, r_lg], writes=[r_dtmp])
            fw.op(fw.dve, lambda e: e.tensor_tensor(out=dtmp[:, 0, :], in0=dtmp[:, 0, :], in1=tab[:, 1, :], op=ALU.mult), reads=[r_tab], writes=[r_dtmp])
            fw.op(fw.dve, lambda e: e.tensor_tensor(out=dtmp[:, 1, :], in0=dtmp[:, 1, :], in1=tab[:, 3, :], op=ALU.mult), reads=[r_tab], writes=[r_dtmp])
            fw.op(fw.dve, lambda e: e.tensor_tensor(out=DT[:, h, :], in0=dtmp[:, 0, :], in1=dtmp[:, 1, :], op=ALU.add), reads=[r_dtmp], writes=[r_DT])
            fw.op(fw.act, lambda e: e.activation(out=qrow[:, h, 0, :], in_=tab[:, 4, :], func=AF.Exp, scale=lf), reads=[r_tab, r_lg], writes=[r_qrow])
            fw.op(fw.act, lambda e: e.activation(out=qrow[:, h, 1, :], in_=tab[:, 5, :], func=AF.Exp, scale=lb), reads=[r_tab, r_lg], writes=[r_qrow])
            fw.op(fw.dve, lambda e: e.tensor_scalar(out=qrow[:, h], in0=qrow[:, h], scalar1=SCALE, scalar2=None, op0=ALU.mult), reads=[], writes=[r_qrow])
            fw.op(fw.act, lambda e: e.activation(out=kd[:, h, 0:1], in_=pcol[:, 0:1], func=AF.Exp, scale=lf), reads=[r_tab, r_lg], writes=[r_kd])
            fw.op(fw.act, lambda e: e.activation(out=kd[:, h, 1:2], in_=pcol[:, 1:2], func=AF.Exp, scale=lb), reads=[r_tab, r_lg], writes=[r_kd])
            fw.op(fw.act, lambda e: e.activation(out=kd[:, h, 2:3], in_=pcol[:, 2:3], func=AF.Exp, scale=lf), reads=[r_tab, r_lg], writes=[r_kd])
            fw.op(fw.act, lambda e: e.activation(out=kd[:, h, 3:4], in_=pcol[:, 2:3], func=AF.Exp, scale=lb), reads=[r_tab, r_lg], writes=[r_kd])

        def load_head(h, s):
            fw.dma(fw.sp, f"rt_in{s}", [
                (qT[:, s, :], C.rqT_s[h]), (kT[:, s, :], C.rkT_s[h]),
                (vv[:, s], C.rv_s[:, h * 256:(h + 1) * 256].rearrange("(t p) e -> p t e", p=128)),
                (gg[:, s], C.sg_s[:, h * 256:(h + 1) * 256].rearrange("(t p) e -> p t e", p=128))],
                reads=[C.proj_res], writes=[r_in[s]])

        border = [1, 0] + list(range(17, 1, -1))
        load_head(0, 0)
        un = 0
        for h in range(8):
            s = h % 2
            if h + 1 < 8:
                load_head(h + 1, (h + 1) % 2)
            fw.op(fw.dve, lambda e: e.memset(S[:], 0.0), writes=[r_S])
            def stage_a(i):
                t = border[i]
                tk = slice(t * 128, (t + 1) * 128)
                fw.mm(r_pk, [(pk[:, 0:128], kT[:, s, tk], C.ident[:], True, True)], reads=[r_in[s], C.ident_res], transpose=True)
                fw.op(fw.dve, lambda e: e.tensor_scalar(out=kf[:, t, :], in0=pk[:, 0:128], scalar1=kd[:, h, 0:1], scalar2=None, op0=ALU.mult),
                      reads=[r_pk, r_kd], writes=[r_kf])
                fw.op(fw.dve, lambda e: e.tensor_scalar(out=kb[:, i % 2, :], in0=pk[:, 0:128], scalar1=kd[:, h, 1:2], scalar2=None, op0=ALU.mult),
                      reads=[r_pk, r_kd], writes=[r_kb[i % 2]])

            stage_a(0)
            for i, t in enumerate(border):
                if i + 1 < len(border):
                    stage_a(i + 1)
                fw.op(fw.act, lambda e: e.activation(out=Sbb[:, t, :], in_=S[:, 1, :], func=AF.Copy), reads=[r_S], writes=[r_Sbb])
                b = un % 2
                un += 1
                fw.mm(r_pkv[b], [(pkv[b][:, 0:256], kb[:, i % 2, :], vv[:, s, t, :], True, True)], reads=[r_kb[i % 2], r_in[s]])
                fw.op(fw.dve, lambda e: e.scalar_tensor_tensor(out=S[:, 1, :], in0=S[:, 1, :], scalar=kd[:, h, 3:4], in1=pkv[b][:, 0:256],
                                                                op0=ALU.mult, op1=ALU.add), reads=[r_pkv[b], r_kd], writes=[r_S])
            for t in range(NT):
                tk = slice(t * 128, (t + 1) * 128)
                u = t % 2
                need_out = t in out_tiles
                fw.op(fw.act, lambda e: e.activation(out=Sfb[:, u, :], in_=S[:, 0, :], func=AF.Copy), reads=[r_S], writes=[r_Sfb[u]])
                if need_out:
                    fw.mm(r_psc, [(psc[:, 0:128], kT[:, s, tk], qT[:, s, tk], True, True)], reads=[r_in[s]])
                    fw.op(fw.dve, lambda e: e.tensor_tensor(out=A[:, u, :], in0=psc[:, 0:128], in1=DT[:, h, :], op=ALU.mult),
                          reads=[r_psc, r_DT], writes=[r_A[u]])
                    fw.op(fw.dve, lambda e: e.tensor_tensor(out=qd[:, u, 0, :], in0=qT[:, s, tk], in1=qrow[:, h, 0, :], op=ALU.mult),
                          reads=[r_in[s], r_qrow], writes=[r_qd[u]])
                    fw.op(fw.dve, lambda e: e.tensor_tensor(out=qd[:, u, 1, :], in0=qT[:, s, tk], in1=qrow[:, h, 1, :], op=ALU.mult),
                          reads=[r_in[s], r_qrow], writes=[r_qd[u]])
                b = un % 2
                un += 1
                fw.mm(r_pkv[b], [(pkv[b][:, 0:256], kf[:, t, :], vv[:, s, t, :], True, True)], reads=[r_kf, r_in[s]])
                fw.op(fw.dve, lambda e: e.scalar_tensor_tensor(out=S[:, 0, :], in0=S[:, 0, :], scalar=kd[:, h, 2:3], in1=pkv[b][:, 0:256],
                                                                op0=ALU.mult, op1=ALU.add), reads=[r_pkv[b], r_kd], writes=[r_S])
                if need_out:
                    fw.mm(r_py[u], [(py[u][:, 0:256], A[:, u, :], vv[:, s, t, :], True, False),
                                    (py[u][:, 0:256], qd[:, u, 0, :], Sfb[:, u, :], False, False),
                                    (py[u][:, 0:256], qd[:, u, 1, :], Sbb[:, t, :], False, True)],
                          reads=[r_A[u], r_qd[u], r_in[s], r_Sfb[u], r_Sbb])
                if not need_out:
                    continue
                fw.op(fw.dve, lambda e: e.bn_stats(out=st[:, u, 0:6], in_=py[u][:, 0:256]), reads=[r_py[u]], writes=[r_st[u]])
                fw.op(fw.dve, lambda e: e.bn_aggr(out=st[:, u, 6:8], in_=st[:, u, 0:6]), reads=[r_st[u]], writes=[r_st[u]])
                fw.op(fw.dve, lambda e: e.tensor_scalar(out=st[:, u, 8:9], in0=st[:, u, 7:8], scalar1=EPS, scalar2=None, op0=ALU.add),
                      reads=[r_st[u]], writes=[r_st[u]])
                fw.op(fw.act, lambda e: e.activation(out=st[:, u, 9:10], in_=st[:, u, 8:9], func=AF.Sqrt), reads=[r_st[u]], writes=[r_st[u]])
                fw.op(fw.dve, lambda e: e.reciprocal(out=st[:, u, 10:11], in_=st[:, u, 9:10]), reads=[r_st[u]], writes=[r_st[u]])
                fw.op(fw.dve, lambda e: e.tensor_scalar(out=yn[:, u, :], in0=py[u][:, 0:256], scalar1=st[:, u, 6:7], scalar2=st[:, u, 10:11],
                                                        op0=ALU.subtract, op1=ALU.mult), reads=[r_py[u], r_st[u]], writes=[r_yn[u]])
                fw.op(fw.dve, lambda e: e.tensor_tensor(out=yn[:, u, :], in0=yn[:, u, :], in1=gn[:, h, :], op=ALU.mult), reads=[r_gn], writes=[r_yn[u]])
                fw.op(fw.dve, lambda e: e.tensor_tensor(out=yb[:, u, :], in0=yn[:, u, :], in1=gg[:, s, t, :], op=ALU.mult),
                      reads=[r_yn[u], r_in[s]], writes=[r_yb[u]])
                fw.mm(r_pt, [(pt[:, j * 128:(j + 1) * 128], yb[:, u, j * 128:(j + 1) * 128], C.ident[:], True, True) for j in range(2)],
                      reads=[r_yb[u], C.ident_res], transpose=True)
                fw.op(fw.act, lambda e: e.activation(out=ro[:, s, :, t, :], in_=pt[:, 0:256].rearrange("p (j c) -> p j c", j=2), func=AF.Copy),
                      reads=[r_pt], writes=[r_ro[s]])
            tl = sorted(out_tiles)
            t0, t1_ = tl[0], tl[-1] + 1
            fw.dma(fw.sp, f"rt_o{s}", [(C.mixT_s[t0:t1_, :, 16 + 2 * h + j, :].rearrange("t p c -> p t c"), ro[:, s, j, t0:t1_, :]) for j in range(2)],
                   reads=[r_ro[s]], writes=[C.mix_res[t] for t in tl])


def phase_att(C):
    nc, fw = C.nc, C.fw
    li = C.li
    with ExitStack() as _es:
        mprev = _es.enter_context(nc.sbuf_tensor(U("at_mp"), [128, 512], BF16))
        mnext = _es.enter_context(nc.sbuf_tensor(U("at_mn"), [128, 512], BF16))
        ones = _es.enter_context(nc.sbuf_tensor(U("at_ones"), [128, 128], BF16))
        f1 = _es.enter_context(nc.sbuf_tensor(U("at_f1"), [128, 128], F32))
        sk = _es.enter_context(nc.sbuf_tensor(U("at_sk"), [128, 16], F32))
        es = _es.enter_context(nc.sbuf_tensor(U("at_es"), [128, 16, 128], F32))
        kT = _es.enter_context(nc.sbuf_tensor(U("at_k"), [128, 2, TOK], BF16))
        qT = _es.enter_context(nc.sbuf_tensor(U("at_q"), [128, 2, 4, TOK], BF16))
        vv = _es.enter_context(nc.sbuf_tensor(U("at_v"), [128, 2, NT, 128], BF16))
        pT = _es.enter_context(nc.sbuf_tensor(U("at_p"), [128, 3, 512], BF16))
        dn = _es.enter_context(nc.sbuf_tensor(U("at_dn"), [128, 2, 512], F32))
        ao = _es.enter_context(nc.sbuf_tensor(U("at_o"), [128, 2, 4, NT, 128], BF16))
        s0 = _es.enter_context(nc.psum_tensor(U("at_s0"), [128, 512], F32))
        s1 = _es.enter_context(nc.psum_tensor(U("at_s1"), [128, 512], F32))
        s2 = _es.enter_context(nc.psum_tensor(U("at_s2"), [128, 512], F32))
        O0 = _es.enter_context(nc.psum_tensor(U("at_O0"), [128, 512], F32))
        O1 = _es.enter_context(nc.psum_tensor(U("at_O1"), [128, 512], F32))
        D0 = _es.enter_context(nc.psum_tensor(U("at_D0"), [128, 512], F32))
        D1 = _es.enter_context(nc.psum_tensor(U("at_D1"), [128, 512], F32))
        r_c, r_sk, r_es = Res(ro=True), Res(), Res()
        r_in = [Res(), Res()]
        r_pT = [Res() for _ in range(3)]
        r_ps = [Res() for _ in range(3)]
        r_O, r_D, r_dn, r_ao = [Res(), Res()], [Res(), Res()], [Res(), Res()], [Res(), Res()]
        pss, OO, DD = [s0, s1, s2], [O0, O1], [D0, D1]
        fw.dma(fw.sp, "at_c", [(mprev[:], C.k_mprev), (mnext[:], C.k_mnext), (ones[:], C.k_onesb)], writes=[r_c])
        fw.dma(fw.sp, "at_sk", [(sk[:], C.attn_sink[li].partition_broadcast(128))], writes=[r_sk])
        fw.op(fw.act, lambda e: e.activation(out=sk[:], in_=sk[:], func=AF.Exp), reads=[], writes=[r_sk])
        fw.op(fw.dve, lambda e: e.memset(f1[:], 1.0), writes=[r_es])
        for hh in range(16):
            fw.op(fw.dve, lambda e: e.tensor_scalar(out=es[:, hh, :], in0=f1[:], scalar1=sk[:, hh:hh + 1], scalar2=None, op0=ALU.mult),
                  reads=[r_sk], writes=[r_es])

        def load_g(g, s):
            fw.dma(fw.sp, f"at_in{s}", [
                (kT[:, s, :], C.akT_s[g]),
                (qT[:, s], C.aqT_s[4 * g:4 * g + 4].rearrange("h p t -> p h t")),
                (vv[:, s], C.av_s[:, g * 128:(g + 1) * 128].rearrange("(t p) d -> p t d", p=128))],
                reads=[C.proj_res], writes=[r_in[s]])

        load_g(0, 0)
        un = 0
        qn = 0
        tl = sorted(C.tiles)
        for g in range(4):
            s = g % 2
            if g + 1 < 4:
                load_g(g + 1, (g + 1) % 2)
            for t in tl:
                if t < 2:
                    keys = [(0, None), (1, None)]
                else:
                    keys = []
                    if t - 1 >= 2:
                        keys.append((t - 1, mprev))
                    keys.append((t, None))
                    if t + 1 < NT:
                        keys.append((t + 1, mnext))
                    keys += [(0, None), (1, None)]
                o = qn % 2
                qn += 1
                tq = slice(t * 128, (t + 1) * 128)
                for ki, (j, mask) in enumerate(keys):
                    b = un % 3
                    un += 1
                    fw.mm(r_ps[b], [(pss[b][:], kT[:, s, j * 128:(j + 1) * 128], qT[:, s, :, tq], True, True)], reads=[r_in[s]])
                    psb = pss[b]
                    fw.op(fw.act, lambda e: e.activation(out=pT[:, b, :], in_=psb[:], func=AF.Exp, scale=SCALE), reads=[r_ps[b]], writes=[r_pT[b]])
                    if mask is not None:
                        fw.op(fw.dve, lambda e: e.tensor_tensor(out=pT[:, b, :], in0=pT[:, b, :], in1=mask[:], op=ALU.mult), reads=[r_c], writes=[r_pT[b]])
                    first, lastk = ki == 0, ki == len(keys) - 1
                    fw.mm(r_O[o], [(OO[o][:], vv[:, s, j, :], pT[:, b, :], first, lastk)], reads=[r_pT[b], r_in[s]])
                    fw.mm(r_D[o], [(DD[o][:], ones[:], pT[:, b, :], first, lastk)], reads=[r_pT[b], r_c])
                fw.op(fw.dve, lambda e: e.tensor_tensor(out=dn[:, o, :], in0=DD[o][:], in1=es[:, 4 * g:4 * g + 4, :].rearrange("p h c -> p (h c)"), op=ALU.add),
                      reads=[r_D[o], r_es], writes=[r_dn[o]])
                fw.op(fw.dve, lambda e: e.reciprocal(out=dn[:, o, :], in_=dn[:, o, :]), reads=[], writes=[r_dn[o]])
                fw.op(fw.dve, lambda e: e.tensor_tensor(out=ao[:, s, :, t, :], in0=OO[o][:].rearrange("p (h c) -> p h c", h=4),
                                                        in1=dn[:, o, :].rearrange("p (h c) -> p h c", h=4), op=ALU.mult),
                      reads=[r_O[o], r_dn[o]], writes=[r_ao[s]])
            t0, t1_ = tl[0], tl[-1] + 1
            fw.dma(fw.sp, f"at_o{s}", [(C.mixT_s[t0:t1_, :, 4 * g + hq, :].rearrange("t p c -> p t c"), ao[:, s, hq, t0:t1_, :]) for hq in range(4)],
                   reads=[r_ao[s]], writes=[C.mix_res[t] for t in tl])


def tile_blocks(tiles, n):
    return [tiles[i:i + n] for i in range(0, len(tiles), n)]


def phase_proj_out(C, wname, src_s, src_res):
    nc, fw = C.nc, C.fw
    li = C.li
    wsrc = C.wb[wname]
    blocks = tile_blocks(C.tiles, 9)
    with ExitStack() as _es:
        ab = _es.enter_context(nc.sbuf_tensor(U("po_a"), [128, 9, KC, 128], BF16))
        wt = _es.enter_context(nc.sbuf_tensor(U("po_w"), [128, 2, KC, 512], BF16))
        stg = _es.enter_context(nc.sbuf_tensor(U("po_st"), [128, 4, 512], F32))
        junk = _es.enter_context(nc.sbuf_tensor(U("po_junk"), [128, 512], BF16))
        p0 = _es.enter_context(nc.psum_tensor(U("po_p0"), [128, 512], F32))
        p1 = _es.enter_context(nc.psum_tensor(U("po_p1"), [128, 512], F32))
        p2 = _es.enter_context(nc.psum_tensor(U("po_p2"), [128, 512], F32))
        p3 = _es.enter_context(nc.psum_tensor(U("po_p3"), [128, 512], F32))
        pp = [p0, p1, p2, p3]
        r_p = [Res() for _ in range(4)]
        r_a, r_junk = Res(), Res()
        r_w = [Res(), Res()]
        r_stg = [Res() for _ in range(4)]
        wn = 0
        un = 0

        def load_w(cb, slot):
            fw.dma(fw.sp, f"po_w{slot}", [(wt[:, slot], wsrc[li][:, cb * 512:(cb + 1) * 512].rearrange("(k p) n -> p k n", p=128))],
                   reads=C.wres[wname][li], writes=[r_w[slot]])

        for blk in blocks:
            fw.dma(fw.sp, "po_a", [(ab[:, i], src_s[t]) for i, t in enumerate(blk)], reads=[src_res[t] for t in blk], writes=[r_a])
            load_w(0, wn % 2)
            for cb in range(8):
                slot = wn % 2
                wn += 1
                if cb + 1 < 8:
                    load_w(cb + 1, wn % 2)
                for i, t in enumerate(blk):
                    b = un % 4
                    un += 1
                    fw.mm(r_p[b], [(pp[b][:], ab[:, i, kperm(wname, kc), :], wt[:, slot, kc, :], kc == 0, kc == KC - 1) for kc in range(KC)],
                          reads=[r_w[slot], r_a])
                    ps = pp[b]
                    fw.op(fw.dve, lambda e: e.tensor_copy(out=stg[:, b, :], in_=ps[:]), reads=[r_p[b]], writes=[r_stg[b]])
                    fw.op(fw.act, lambda e: e.activation(out=junk[:], in_=stg[:, b, :], func=AF.Square, accum_out=C.ssq[:, t, cb:cb + 1]),
                          reads=[r_stg[b]], writes=[r_junk, C.ssq_res])
                    fw.dma(fw.sp, f"po_o{b}", [(C.y_s[t * 128:(t + 1) * 128, cb * 512:(cb + 1) * 512], stg[:, b, :])],
                           reads=[r_stg[b]], writes=[C.y_res[t]])


def phase_mlp(C):
    nc, fw = C.nc, C.fw
    li = C.li
    wu, wd = C.wb["w_up"], C.wb["w_down"]
    blocks = tile_blocks(C.tiles, 3)
    if len(blocks[-1]) == 1:
        tail = blocks[-2] + blocks[-1]
        blocks = blocks[:-2] + [tail[:2], tail[2:]]
    with ExitStack() as _es:
        hb = _es.enter_context(nc.sbuf_tensor(U("ml_h"), [128, 3, KC, 128], BF16))
        uT = _es.enter_context(nc.sbuf_tensor(U("ml_u"), [128, 128, 384], BF16))
        wut = _es.enter_context(nc.sbuf_tensor(U("ml_wu"), [128, 2, KC, 256], BF16))
        wdt = _es.enter_context(nc.sbuf_tensor(U("ml_wd"), [128, 2, 16, 512], BF16))
        rl = _es.enter_context(nc.sbuf_tensor(U("ml_r"), [128, 2, 384], BF16))
        stg = _es.enter_context(nc.sbuf_tensor(U("ml_st"), [128, 2, 512], F32))
        junk = _es.enter_context(nc.sbuf_tensor(U("ml_junk"), [128, 512], BF16))
        p0 = _es.enter_context(nc.psum_tensor(U("ml_p0"), [128, 512], F32))
        p1 = _es.enter_context(nc.psum_tensor(U("ml_p1"), [128, 512], F32))
        a0 = _es.enter_context(nc.psum_tensor(U("ml_a0"), [128, 512], F32))
        a1 = _es.enter_context(nc.psum_tensor(U("ml_a1"), [128, 512], F32))
        a2 = _es.enter_context(nc.psum_tensor(U("ml_a2"), [128, 512], F32))
        a3 = _es.enter_context(nc.psum_tensor(U("ml_a3"), [128, 512], F32))
        a4 = _es.enter_context(nc.psum_tensor(U("ml_a4"), [128, 512], F32))
        a5 = _es.enter_context(nc.psum_tensor(U("ml_a5"), [128, 512], F32))
        pu = [p0, p1]
        acc = [[a0, a1, a2], [a3, a4, a5]]
        r_pu = [Res(), Res()]
        r_acc = [[Res() for _ in range(3)] for _ in range(2)]
        r_h, r_u, r_junk = Res(), Res(), Res()
        r_wu, r_wd, r_rl, r_stg = [Res(), Res()], [Res(), Res()], [Res(), Res()], [Res(), Res()]
        nwu = 0
        nwd = 0
        un = 0
        sn = 0
        cbn = 0

        def load_wu(fg, slot):
            fw.dma(fw.sp, f"ml_wu{slot}", [(wut[:, slot], wu[li][:, fg * 256:(fg + 1) * 256].rearrange("(k p) n -> p k n", p=128))],
                   reads=C.wres["w_up"][li], writes=[r_wu[slot]])

        def load_wd(cb, fgrp, slot):
            fw.dma(fw.sp, f"ml_wd{slot}", [(wdt[:, slot], wd[li][fgrp * 2048:(fgrp + 1) * 2048, cb * 512:(cb + 1) * 512].rearrange("(k p) n -> p k n", p=128))],
                   reads=C.wres["w_down"][li], writes=[r_wd[slot]])

        for blk in blocks:
            nt_ = len(blk)
            N = nt_ * 128
            fw.dma(fw.sp, "ml_h", [(hb[:, i], C.h2T_s[t]) for i, t in enumerate(blk)], reads=[C.h2T_res[t] for t in blk], writes=[r_h])
            load_wu(0, nwu % 2)
            for fg in range(64):
                slot = nwu % 2
                nwu += 1
                if fg + 1 < 64:
                    load_wu(fg + 1, nwu % 2)
                for j in range(2):
                    b = un % 2
                    un += 1
                    fw.mm(r_pu[b], [(pu[b][:, 0:N], wut[:, slot, kc, j * 128:(j + 1) * 128], hb[:, 0:nt_, kperm("w_up", kc), :], kc == 0, kc == KC - 1) for kc in range(KC)],
                          reads=[r_wu[slot], r_h])
                    ps = pu[b]
                    fw.op(fw.act, lambda e: e.activation(out=rl[:, b, 0:N], in_=ps[:, 0:N], func=AF.Relu), reads=[r_pu[b]], writes=[r_rl[b]])
                    fw.op(fw.dve, lambda e: e.tensor_tensor(out=uT[:, fg * 2 + j, 0:N], in0=rl[:, b, 0:N], in1=rl[:, b, 0:N], op=ALU.mult),
                          reads=[r_rl[b]], writes=[r_u])
            load_wd(0, 0, nwd % 2)
            for cb in range(8):
                aset = cbn % 2
                cbn += 1
                for fgrp in range(8):
                    slot = nwd % 2
                    nwd += 1
                    nxt = (cb, fgrp + 1) if fgrp + 1 < 8 else ((cb + 1, 0) if cb + 1 < 8 else None)
                    if nxt is not None:
                        load_wd(nxt[0], nxt[1], nwd % 2)
                    for i in range(nt_):
                        fw.mm(r_acc[aset][i], [(acc[aset][i][:], uT[:, kperm("w_down", fgrp * 16 + k), i * 128:(i + 1) * 128], wdt[:, slot, k, :],
                                                fgrp == 0 and k == 0, fgrp == 7 and k == 15) for k in range(16)],
                              reads=[r_wd[slot], r_u])
                for i, t in enumerate(blk):
                    sb = sn % 2
                    sn += 1
                    ps = acc[aset][i]
                    fw.op(fw.dve, lambda e: e.tensor_copy(out=stg[:, sb, :], in_=ps[:]), reads=[r_acc[aset][i]], writes=[r_stg[sb]])
                    fw.op(fw.act, lambda e: e.activation(out=junk[:], in_=stg[:, sb, :], func=AF.Square, accum_out=C.ssq[:, t, cb:cb + 1]),
                          reads=[r_stg[sb]], writes=[r_junk, C.ssq_res])
                    fw.dma(fw.sp, f"ml_o{sb}", [(C.y_s[t * 128:(t + 1) * 128, cb * 512:(cb + 1) * 512], stg[:, sb, :])],
                           reads=[r_stg[sb]], writes=[C.y_res[t]])


_CONSTS = None


def make_in_maps(inputs, cores):
    global _CONSTS
    if _CONSTS is None:
        _CONSTS = _consts()
    K = _CONSTS
    f = np.float32
    A = lambda a: np.ascontiguousarray(a, dtype=f)
    x, c, ctx, c_ctx = inputs["x"], inputs["c"], inputs["ctx"], inputs["c_ctx"]
    norm_g = A(inputs["norm_g"])
    norm_gT = A(norm_g.reshape(DEPTH, 4, KC, 128).transpose(0, 1, 3, 2))
    c9 = np.concatenate([np.asarray(c, dtype=f), np.asarray(c_ctx, dtype=f)[None, :]], 0)
    c9T = A(c9.T.reshape(KC, 128, 9).transpose(1, 0, 2))
    w_ada = np.asarray(inputs["w_ada"]).reshape(DEPTH, D, 6, 8, 512)
    b_ada = np.asarray(inputs["b_ada"]).reshape(DEPTH, 6, 8, 512)
    shared = dict(norm_g=norm_g, norm_gT=norm_gT, attn_sink=A(inputs["attn_sink"]),
                  ret_log_decay=A(np.asarray(inputs["ret_log_decay"]).reshape(DEPTH, 16)),
                  ret_gn_g=A(inputs["ret_gn_g"]), c9T=c9T, **K)
    maps = []
    for b in cores:
        sel = np.zeros((9, 2), f)
        sel[b, 0] = 1.0
        sel[8, 1] = 1.0
        m = dict(shared)
        m.update(x=A(x[b]), ctx=A(ctx[b]), sel=sel,
                 w_ada_sh=A(w_ada[:, :, :, b, :].reshape(DEPTH, D, 3072)), b_ada_sh=A(b_ada[:, :, b, :].reshape(DEPTH, 3072)),
                 w_in_sh=A(inputs["w_in"][:, b * 512:(b + 1) * 512, :]), w_out_sh=A(inputs["w_out"][:, b * 512:(b + 1) * 512, :]),
                 w_up_sh=A(inputs["w_up"][:, b * 512:(b + 1) * 512, :]), w_down_sh=A(inputs["w_down"][:, b * 2048:(b + 1) * 2048, :]))
        maps.append(m)
    return maps


_NC = None


def kernel(**inputs):
    global _NC
    if _NC is None:
        _NC = build()
    maps = make_in_maps(inputs, list(range(8)))
    res = run_bass_kernel_spmd(_NC, maps, core_ids=list(range(8)))
    return np.stack([np.asarray(r["out"], dtype=np.float32) for r in res.results], 0)
```
